# Optimizing a Trainium2 kernel written in Bass

```python
import math
import jax
import jax.numpy as jnp
from jax import lax
import numpy as np

D_MODEL = 1024
BATCH = 8
SEQ = 4096
DEPTH = 1

GRID_W = 64
CTX_LEN = 256
D_MIX = D_MODEL
D_CONV = D_MIX // 2
CONV_HEADS = 8
CONV_HEAD_DIM = D_CONV // CONV_HEADS
D_CONV_H = (CONV_HEADS // 2) * CONV_HEAD_DIM
D_SSM = D_MIX - D_CONV
SSM_GROUP = 16
SSM_GROUPS = D_SSM // SSM_GROUP
SSM_STATE = 64
D_IN = 3 * D_CONV + D_SSM
D_FF = 2816
N_MOD = 9
EPS = 1e-6
DT_MIN = 1e-3
DT_MAX = 1e-1

kernel_name = "hymba_style_conv_s5_macaron_dit_block"


def rms_norm(x, g):
    xf = x.astype(jnp.float32)
    y = xf * lax.rsqrt(jnp.mean(xf * xf, axis=-1, keepdims=True) + EPS)
    return (y * g.astype(jnp.float32)).astype(x.dtype)


def modulate(h, shift, scale):
    return h * (1 + scale) + shift


def adaln(cond, w, b):
    m = jax.nn.silu(cond) @ w + b
    return [t[..., None, :] for t in jnp.split(m, N_MOD, axis=-1)]


def half_step_ffn(x, mods, g_pre, g_post, w_gu, w_down):
    shift, scale, gate = mods
    h = modulate(rms_norm(x, g_pre), shift, scale)
    g, u = jnp.split(h @ w_gu, 2, axis=-1)
    y = (jax.nn.silu(g) * u) @ w_down
    return x + 0.5 * gate * rms_norm(y, g_post)


def shift_conv3(v, w, axis):
    n = v.shape[axis]
    pad = [(0, 0)] * v.ndim
    pad[axis] = (1, 1)
    vp = jnp.pad(v, pad)
    prev = lax.slice_in_dim(vp, 0, n, axis=axis)
    nxt = lax.slice_in_dim(vp, 2, n + 2, axis=axis)
    return w[0] * prev + w[1] * v + w[2] * nxt


def grid_short_conv(v, w):
    b, l, ch = v.shape
    rows = l // GRID_W
    vg = v.reshape(b, rows, GRID_W, ch)
    horiz = shift_conv3(vg[..., :D_CONV_H], w[:, :D_CONV_H], axis=2)
    vert = shift_conv3(vg[..., D_CONV_H:], w[:, D_CONV_H:], axis=1)
    return jnp.concatenate([horiz, vert], axis=-1).reshape(b, l, ch)


def s5_discretise(lam_re, lam_im, log_dt, b_re, b_im, c_re, c_im):
    lam = lax.complex(lam_re.astype(jnp.float32), lam_im.astype(jnp.float32))
    dt = jnp.exp(log_dt.astype(jnp.float32))[:, None]
    lam_bar = jnp.exp(lam * dt)
    b = lax.complex(b_re.astype(jnp.float32), b_im.astype(jnp.float32))
    b_bar = ((lam_bar - 1.0) / lam)[:, :, None] * b
    c_mat = lax.complex(c_re.astype(jnp.float32), c_im.astype(jnp.float32))
    return lam_bar, b_bar, c_mat


def _linear_recurrence(e1, e2):
    a1, b1 = e1
    a2, b2 = e2
    return a1 * a2, a2 * b1 + b2


def s5_scan(ug, lam_bar, b_bar, h0, reverse):
    bu = jnp.einsum('blgh,gph->blgp', ug.astype(jnp.complex64), b_bar)
    first = bu.shape[1] - 1 if reverse else 0
    bu = bu.at[:, first].add(lam_bar * h0)
    a = jnp.broadcast_to(lam_bar, (1,) + bu.shape[1:])
    _, h = lax.associative_scan(_linear_recurrence, (a, bu), axis=1, reverse=reverse)
    return h


def s5_states(u, disc, h0s):
    b, l, _ = u.shape
    ug = u.astype(jnp.float32).reshape(b, l, SSM_GROUPS, SSM_GROUP)
    h_fwd = s5_scan(ug, disc[0][0], disc[0][1], h0s[0], reverse=False)
    h_bwd = s5_scan(ug, disc[1][0], disc[1][1], h0s[1], reverse=True)
    return (h_fwd, h_bwd)


def s5_readout(u, states, disc, d_skip, w_glu, b_glu):
    b, l, _ = u.shape
    ug = u.astype(jnp.float32).reshape(b, l, SSM_GROUPS, SSM_GROUP)
    y = (jnp.einsum('blgp,ghp->blgh', states[0], disc[0][2]).real
         + jnp.einsum('blgp,ghp->blgh', states[1], disc[1][2]).real
         + d_skip.astype(jnp.float32).reshape(SSM_GROUPS, SSM_GROUP) * ug)
    y = jax.nn.gelu(y.reshape(b, l, D_SSM)).astype(u.dtype)
    return y * jax.nn.sigmoid(y @ w_glu + b_glu)


def setup_inputs(seed: int = 0) -> dict:
    key = jax.random.key(seed)
    ks = iter(jax.random.split(key, 40))
    f32 = jnp.float32

    def nrm(shape, scale):
        return scale * jax.random.normal(next(ks), shape, f32)

    def gain(shape):
        return 1.0 + nrm(shape, 0.05)

    L = DEPTH
    G, P, H = SSM_GROUPS, SSM_STATE, SSM_GROUP
    return {
        'x': nrm((BATCH, SEQ, D_MODEL), 1.0),
        'c': nrm((BATCH, D_MODEL), 1.0),
        'ctx': nrm((BATCH, CTX_LEN, D_MODEL), 1.0),
        'c_ctx': nrm((D_MODEL,), 1.0),
        'w_ada': nrm((L, D_MODEL, N_MOD * D_MODEL), 0.5 * D_MODEL ** -0.5),
        'b_ada': nrm((L, N_MOD * D_MODEL), 0.02),
        'ffn1_g_pre': gain((L, D_MODEL)),
        'ffn1_g_post': gain((L, D_MODEL)),
        'ffn1_w_gu': nrm((L, D_MODEL, 2 * D_FF), D_MODEL ** -0.5),
        'ffn1_w_down': nrm((L, D_FF, D_MODEL), D_FF ** -0.5),
        'mix_g_pre': gain((L, D_MODEL)),
        'mix_g_post': gain((L, D_MODEL)),
        'w_in': nrm((L, D_MODEL, D_IN), D_MODEL ** -0.5),
        'conv_w': nrm((L, 3, D_CONV), 3 ** -0.5),
        'ssm_lam_re': -0.5 + nrm((L, 2, G, P), 0.01),
        'ssm_lam_im': math.pi * jnp.arange(P, dtype=f32) + nrm((L, 2, G, P), 0.01),
        'ssm_log_dt': jax.random.uniform(next(ks), (L, 2, G), f32, math.log(DT_MIN), math.log(DT_MAX)),
        'ssm_b_re': nrm((L, 2, G, P, H), H ** -0.5),
        'ssm_b_im': nrm((L, 2, G, P, H), H ** -0.5),
        'ssm_c_re': nrm((L, 2, G, H, P), 0.5),
        'ssm_c_im': nrm((L, 2, G, H, P), 0.5),
        'ssm_d': nrm((L, D_SSM), 0.5),
        'w_glu': nrm((L, D_SSM, D_SSM), D_SSM ** -0.5),
        'b_glu': nrm((L, D_SSM), 0.02),
        'w_out': nrm((L, D_MIX, D_MODEL), D_MIX ** -0.5),
        'ffn2_g_pre': gain((L, D_MODEL)),
        'ffn2_g_post': gain((L, D_MODEL)),
        'ffn2_w_gu': nrm((L, D_MODEL, 2 * D_FF), D_MODEL ** -0.5),
        'ffn2_w_down': nrm((L, D_FF, D_MODEL), D_FF ** -0.5),
    }


def reference(x, c, ctx, c_ctx, w_ada, b_ada,
              ffn1_g_pre, ffn1_g_post, ffn1_w_gu, ffn1_w_down,
              mix_g_pre, mix_g_post, w_in, conv_w,
              ssm_lam_re, ssm_lam_im, ssm_log_dt, ssm_b_re, ssm_b_im, ssm_c_re, ssm_c_im,
              ssm_d, w_glu, b_glu, w_out,
              ffn2_g_pre, ffn2_g_post, ffn2_w_gu, ffn2_w_down):
    splits = [D_CONV, 2 * D_CONV, 3 * D_CONV]
    for i in range(DEPTH):
        last = i == DEPTH - 1
        mods = adaln(c, w_ada[i], b_ada[i])
        mods_c = adaln(c_ctx, w_ada[i], b_ada[i])

        x = half_step_ffn(x, mods[0:3], ffn1_g_pre[i], ffn1_g_post[i], ffn1_w_gu[i], ffn1_w_down[i])
        ctx = half_step_ffn(ctx, mods_c[0:3], ffn1_g_pre[i], ffn1_g_post[i], ffn1_w_gu[i], ffn1_w_down[i])

        disc = [s5_discretise(ssm_lam_re[i, d], ssm_lam_im[i, d], ssm_log_dt[i, d],
                              ssm_b_re[i, d], ssm_b_im[i, d], ssm_c_re[i, d], ssm_c_im[i, d])
                for d in range(2)]

        pc = modulate(rms_norm(ctx, mix_g_pre[i]), mods_c[3], mods_c[4]) @ w_in[i]
        bg_c, cg_c, v_c, u_c = jnp.split(pc, splits, axis=-1)
        zero = jnp.zeros((ctx.shape[0], SSM_GROUPS, SSM_STATE), jnp.complex64)
        st_c = s5_states(u_c, disc, (zero, zero))
        h0s = (st_c[0][:, -1], st_c[1][:, 0])

        p = modulate(rms_norm(x, mix_g_pre[i]), mods[3], mods[4]) @ w_in[i]
        bg, cg, v, u = jnp.split(p, splits, axis=-1)
        y_conv = bg * grid_short_conv(cg * v, conv_w[i])
        y_ssm = s5_readout(u, s5_states(u, disc, h0s), disc, ssm_d[i], w_glu[i], b_glu[i])
        y = jnp.concatenate([y_conv, y_ssm], axis=-1) @ w_out[i]
        x = x + mods[5] * rms_norm(y, mix_g_post[i])

        if not last:
            yc_conv = bg_c * shift_conv3(cg_c * v_c, conv_w[i], axis=1)
            yc_ssm = s5_readout(u_c, st_c, disc, ssm_d[i], w_glu[i], b_glu[i])
            yc = jnp.concatenate([yc_conv, yc_ssm], axis=-1) @ w_out[i]
            ctx = ctx + mods_c[5] * rms_norm(yc, mix_g_post[i])
            ctx = half_step_ffn(ctx, mods_c[6:9], ffn2_g_pre[i], ffn2_g_post[i], ffn2_w_gu[i], ffn2_w_down[i])

        x = half_step_ffn(x, mods[6:9], ffn2_g_pre[i], ffn2_g_post[i], ffn2_w_gu[i], ffn2_w_down[i])
    return x
```

```python
import contextlib
import math
import numpy as np
import concourse.bass as bass
import concourse.mybir as mybir
from concourse.bass_utils import run_bass_kernel_spmd

F32 = mybir.dt.float32
BF16 = mybir.dt.bfloat16
ALU = mybir.AluOpType
AF = mybir.ActivationFunctionType

D = 1024
L = 4096
LC = 256
FF = 2816
KC = 8
FC = 22
NG = 32
EPS = 1e-6
MAGIC = 12582912.0
TWO_PI = 2.0 * math.pi
SB_TOP = 227328
SB_BASE = 16512
import os
NOSELF = os.environ.get('NOSELF', '0') == '1'


class Buf:
    __slots__ = ("name", "W", "R")

    def __init__(self, name):
        self.name = name
        self.W = {}
        self.R = {}


class Prog:
    ENG = ("pe", "act", "dve", "pool", "sp")
    ATTR = {"pe": "tensor", "act": "scalar", "dve": "vector", "pool": "gpsimd", "sp": "sync"}

    def __init__(self, nc, stack):
        self.nc = nc
        self.stack = stack
        self.esem = {e: stack.enter_context(nc.semaphore("s_" + e)) for e in self.ENG}
        self.cnt = {e: 0 for e in self.ENG}
        self.streams = {e: [] for e in self.ENG}
        self.waited = {e: {} for e in self.ENG}
        self.dsems = {}
        self.dcnt = {}
        self.free_dsems = []

    def _dsem(self, key):
        if key not in self.dsems:
            if self.free_dsems:
                s, c = self.free_dsems.pop()
            else:
                s, c = self.stack.enter_context(self.nc.semaphore("d_%d" % len(self.dsems))), 0
            self.dsems[key] = s
            self.dcnt[key] = c
        return self.dsems[key]

    def _need(self, eng, need):
        waits = []
        wd = self.waited[eng]
        for s, v in need.items():
            if wd.get(s, 0) < v:
                wd[s] = v
                waits.append((s, v))
        return waits

    def _deps(self, eng, reads, writes, tok, nowaw=False):
        need = {}

        def add(s, v):
            if need.get(s, 0) < v:
                need[s] = v
        own = self.esem[eng]
        for b in reads:
            for s, v in b.W.items():
                if s is own and (eng in ("pe", "sp") or NOSELF):
                    continue
                add(s, v)
        for b in writes:
            if not nowaw:
                for s, v in b.W.items():
                    if s is not own:
                        add(s, v)
            for s, v in b.R.items():
                if s is not own:
                    add(s, v)
        waits = self._need(eng, need)
        ts, tv = tok
        for b in reads:
            if b.R.get(ts, 0) < tv:
                b.R[ts] = tv
        for b in writes:
            if b.R:
                b.W = {ts: tv}
                b.R = {}
            else:
                b.W[ts] = tv
        return waits

    def op(self, eng, method, *args, reads=(), writes=(), inc=True, nowaw=False, **kw):
        tok = (self.esem[eng], self.cnt[eng] + 1)
        waits = self._deps(eng, reads, writes, tok, nowaw=nowaw)
        if inc:
            self.cnt[eng] += 1
        fn = (lambda e, m=method, a=args, k=kw: getattr(e, m)(*a, **k))
        self.streams[eng].append((waits, fn, self.esem[eng] if inc else None, 1))

    def dma(self, eng, out, in_, reads=(), writes=(), key=None):
        kb = key if key is not None else (writes[0] if writes else reads[0])
        sem = self._dsem(kb)
        self.dcnt[kb] += 16
        tok = (sem, self.dcnt[kb])
        waits = self._deps(eng, reads, writes, tok)
        self.streams[eng].append((waits, lambda e, o=out, i=in_: e.dma_start(out=o, in_=i), sem, 16))

    def barrier(self, keep=()):
        need = {self.esem[e]: self.cnt[e] for e in self.ENG if self.cnt[e] > 0}
        for k, s in self.dsems.items():
            if self.dcnt[k] > 0 and k not in keep:
                need[s] = max(need.get(s, 0), self.dcnt[k])
        for e in self.ENG:
            w = self._need(e, {s: v for s, v in need.items() if s is not self.esem[e]})
            if w:
                self.streams[e].append((w, None, None, 0))
        kept_s = {k: self.dsems[k] for k in self.dsems if k in keep}
        kept_c = {k: self.dcnt[k] for k in self.dsems if k in keep}
        for k in list(self.dsems):
            if k not in keep:
                self.free_dsems.append((self.dsems[k], self.dcnt[k]))
        self.dsems = kept_s
        self.dcnt = kept_c

    def emit(self):
        nc = self.nc
        with nc.Block() as block:
            for e in self.ENG:
                stream = self.streams[e]

                def body(eo, stream=stream):
                    for waits, fn, sem, incv in stream:
                        for s, v in waits:
                            eo.wait_ge(s, v)
                        if fn is None:
                            continue
                        ins = fn(eo)
                        if sem is not None:
                            ins.then_inc(sem, incv)
                getattr(block, self.ATTR[e])(body)
        self.streams = {e: [] for e in self.ENG}


class Ctx:
    pass


class _Stop(Exception):
    pass


import os


def _cut(n):
    if int(os.environ.get('S5STOP', '99')) == n:
        raise _Stop()


def _mk(ap, dims):
    return bass.AP(ap.tensor, ap.offset, [list(ap.ap[0])] + [list(d) for d in dims])


def build(debug=(), inject=(), phases=None):
    nc = bass.Bass("TRN2", target_bir_lowering=False)
    K = Ctx()
    K.nc = nc
    K.uid = 0
    all_ph = ["mods", "ffn1", "s5setup", "m1", "s5", "m2a", "m2b", "m2c", "m2d", "ffn2"]
    phases = all_ph if phases is None else phases

    def din(name, shape):
        return nc.dram_tensor(name, shape, F32, kind="ExternalInput").ap()

    def dscratch(name, shape):
        kind = "ExternalInput" if name in inject else "Internal"
        return nc.dram_tensor(name, shape, F32, kind=kind).ap()

    I = Ctx()
    I.x = din("x", [L, D])
    I.ctx = din("ctx", [LC, D])
    I.cc = din("cc", [128, 16])
    I.w_ada = din("w_ada", [D, 9 * D])
    I.bada = din("bada", [128, 72])
    I.gains = din("gains", [128, 48])
    I.f1_wgu = din("ffn1_w_gu", [D, 2 * FF])
    I.f1_wd = din("ffn1_w_down", [FF, D])
    I.f2_wgu = din("ffn2_w_gu", [D, 2 * FF])
    I.f2_wd = din("ffn2_w_down", [FF, D])
    I.w_in = din("w_in", [D, 2048])
    I.cw = din("cw", [128, 12])
    I.s5a = din("s5a", [128, 96])
    I.s5b = din("s5b", [128, 4 * NG * 16])
    I.dcol = din("dcol", [128, NG])
    I.w_glu = din("w_glu", [512, 512])
    I.bglu = din("bglu", [128, 4])
    I.w_out = din("w_out", [D, D])
    I.cst = din("cst", [128, 128 * 3 + 26 + 544])
    out = nc.dram_tensor("out", [L, D], F32, kind="ExternalOutput").ap()
    x1s = dscratch("x1s", [L, D])
    c1s = dscratch("c1s", [LC, D])
    x2s = dscratch("x2s", [L, D])
    dbg = {}
    for name, shape in debug:
        dbg[name] = nc.dram_tensor("dbg_" + name, shape, F32, kind="ExternalOutput").ap()

    with contextlib.ExitStack() as st:
        p = Prog(nc, st)
        K.p = p

        def sb(name, shape, dtype, off):
            K.uid += 1
            nbytes = int(np.prod(shape[1:])) * mybir.dt.size(dtype)
            assert off % 32 == 0 and off + nbytes <= SB_TOP, (name, off, nbytes)
            return nc.alloc_sbuf_tensor_at("%s_%d" % (name, K.uid), shape, dtype, offset=off)

        class Arena:
            def __init__(self, base, top):
                self.ptr = base
                self.top = top

            def al(self, name, shape, dtype):
                nbytes = int(np.prod(shape[1:])) * mybir.dt.size(dtype)
                nbytes = (nbytes + 31) // 32 * 32
                t = sb(name, shape, dtype, self.ptr)
                self.ptr += nbytes
                assert self.ptr <= self.top, (name, self.ptr, self.top)
                return t

        PA = Arena(SB_BASE, SB_BASE + 6144)
        ident = PA.al("ident", [128, 128], F32)
        identb = PA.al("identb", [128, 128], BF16)
        onesf = PA.al("onesf", [128, 128], F32)
        epsT = PA.al("epsT", [128, 1], F32)
        mhalf = PA.al("mhalf", [128, 16], F32)
        magt = PA.al("magt", [128, 4], F32)
        modsT = PA.al("modsT", [128, 72, 2], F32)
        gains = PA.al("gains", [128, 8, 6], F32)
        gs1 = PA.al("gs1", [128, 8, 2], F32)
        gsm = PA.al("gsm", [128, 8, 2], F32)
        gs2 = PA.al("gs2", [128, 8, 2], F32)
        gc1 = PA.al("gc1", [128, 8, 2], F32)
        gcm = PA.al("gcm", [128, 8, 2], F32)
        gc2 = PA.al("gc2", [128, 8, 2], F32)
        cwt = PA.al("cwt", [128, 4, 3], F32)
        bglut = PA.al("bglut", [128, 4], F32)
        B_const = Buf("const")
        B_mods = Buf("mods")
        ARENA0 = SB_BASE + 6144

        OFF_YMIX = SB_TOP - 65536
        OFF_YTM = OFF_YMIX - 32768
        OFF_MATS = OFF_YTM - 45056
        OFF_BGS = OFF_YTM - 32768

        def dbg_dump(name, src_ap, reads, eng="sp"):
            if name in dbg:
                p.dma(eng, dbg[name], src_ap, reads=reads, key=Buf("dbgk"))

        def shcol(j, kc, t):
            return modsT[:, j * 8 + kc, t:t + 1]

        def phase_init():
            p.dma("sp", ident[:], I.cst[:, 0:128], writes=[B_const])
            p.dma("sp", gains[:].rearrange("p a b -> p (a b)"), I.gains, writes=[B_const])
            p.dma("sp", cwt[:].rearrange("p a b -> p (a b)"), I.cw, writes=[B_const])
            p.dma("sp", bglut[:], I.bglu, writes=[B_const])
            p.op("dve", "tensor_copy", out=identb[:], in_=ident[:], reads=[B_const], writes=[B_const])
            p.op("pool", "memset", onesf[:], 1.0, writes=[B_const])
            p.op("pool", "memset", epsT[:], EPS, writes=[B_const])
            p.op("pool", "memset", mhalf[:], -0.5, writes=[B_const])
            p.op("pool", "memset", magt[:, 0:1], MAGIC, writes=[B_const])
            p.op("pool", "memset", magt[:, 1:2], -MAGIC, writes=[B_const])
            p.op("pool", "memset", magt[:, 2:3], 0.25, writes=[B_const])
            p.barrier()
            p.emit()

        def phase_mods():
            pre = "ffn1" in phases
            K.ffn1_wd = None
            if pre:
                _, K.ffn1_wd = ffn_weight_dmas(I.f1_wgu, I.f1_wd, range(8), True, split=True)
            A = Arena(ARENA0 + (135168 if pre else 0), SB_TOP)
            cct = A.al("cct", [128, 16], F32)
            scc = A.al("scc", [128, 2, 8], F32)
            badat = A.al("badat", [128, 72], F32)
            rr = [A.al("rr", [2, 512], F32)] * 2
            wst = [A.al("wst%d" % i, [128, 8, 1024], F32) for i in range(2)]
            with contextlib.ExitStack() as ps:
                pm = ps.enter_context(nc.psum_tensor("pm", [128, 144], F32))
                prow = [ps.enter_context(nc.psum_tensor("prw%d" % i, [2, 512], F32)) for i in range(2)]
                b_cc, b_scc, b_bada, b_pm, b_g = Buf("cc"), Buf("scc"), Buf("bada"), Buf("pm"), B_const
                b_w = [Buf("wst0"), Buf("wst1")]
                b_prow = [Buf("prow0"), Buf("prow1")]
                b_rr = [Buf("rr0")] * 2
                p.dma("sp", cct[:], I.cc, writes=[b_cc])
                p.dma("sp", badat[:], I.bada, writes=[b_bada])
                p.op("act", "activation", out=scc[:].rearrange("p t k -> p k t"), in_=cct[:].rearrange("p (k t) -> p k t", t=2), func=AF.Silu,
                     reads=[b_cc], writes=[b_scc])
                wv = I.w_ada.rearrange("(kc p) n -> p kc n", p=128)
                k = 0
                for j in range(9):
                    w = wst[j % 2]
                    for kc in range(8):
                        p.dma("sp", w[:, kc, :], wv[:, kc, j * 1024:(j + 1) * 1024], writes=[b_w[j % 2]])
                    for hb in range(2):
                        pr = prow[k % 2]
                        for kc in range(8):
                            p.op("pe", "matmul", pr[:, :], lhsT=scc[:, :, kc], rhs=w[:, kc, hb * 512:(hb + 1) * 512],
                                 start=(kc == 0), stop=(kc == 7), reads=[b_w[j % 2], b_scc], writes=[b_prow[k % 2]], inc=(kc == 7))
                        p.op("act", "copy", out=rr[k % 2][:, :], in_=pr[:, :], reads=[b_prow[k % 2]], writes=[b_rr[k % 2]])
                        for nch in range(4):
                            col = (j * 8 + hb * 4 + nch) * 2
                            p.op("pe", "matmul", pm[:, col:col + 2], lhsT=rr[k % 2][:, nch * 128:(nch + 1) * 128], rhs=ident[0:2, 0:2],
                                 start=True, stop=True, reads=[b_rr[k % 2], B_const], writes=[b_pm])
                        k += 1
                p.op("dve", "tensor_tensor", out=modsT[:], in0=pm[:].rearrange("p (n t) -> p n t", t=2),
                     in1=_mk(badat[:, 0:1], [[1, 72], [0, 2]]), op=ALU.add, reads=[b_pm, b_bada], writes=[B_mods])
                mv = modsT[:].rearrange("p (j k) t -> p j k t", j=9)

                def gcolb(idx):
                    return _mk(gains[:, 0, idx:idx + 1], [[6, 8], [0, 2]])
                for dst, jscale, gi in ((gs1, 1, 0), (gsm, 4, 2), (gs2, 7, 4)):
                    p.op("dve", "scalar_tensor_tensor", out=dst[:], in0=mv[:, jscale], scalar=1.0, in1=gcolb(gi),
                         op0=ALU.add, op1=ALU.mult, reads=[B_mods, b_g], writes=[B_mods])
                for dst, jg, gi, sc in ((gc1, 2, 1, 0.5), (gcm, 5, 3, 1.0), (gc2, 8, 5, 0.5)):
                    p.op("dve", "scalar_tensor_tensor", out=dst[:], in0=mv[:, jg], scalar=sc, in1=gcolb(gi),
                         op0=ALU.mult, op1=ALU.mult, reads=[B_mods, b_g], writes=[B_mods])
                if "modsT" in dbg:
                    p.dma("sp", dbg["modsT"], modsT[:].rearrange("p n t -> p (n t)"), reads=[B_mods], key=Buf("k"))
                p.emit()

        def make_row(dst, col_of_kc, bdst, prow, bpr, dg, bdg):
            for kc in range(8):
                p.op("dve", "tensor_scalar", out=dg[kc % 2][:], in0=ident[:], scalar1=col_of_kc(kc), scalar2=None,
                     op0=ALU.mult, reads=[B_const, B_mods], writes=[bdg[kc % 2]])
                p.op("pe", "matmul", prow[:, kc * 128:(kc + 1) * 128], lhsT=onesf[:], rhs=dg[kc % 2][:], start=True, stop=True,
                     reads=[bdg[kc % 2], B_const], writes=[bpr])
            p.op("act", "copy", out=dst[:], in_=prow[:], reads=[bpr], writes=[bdst])

        def norm_T(xt, npart, nsub, gs, sh_j, t, bufs):
            norm_A(xt, npart, nsub, bufs)
            norm_B(npart, nsub, gs, sh_j, t, bufs)

        def norm_A(xt, npart, nsub, bufs):
            for st_ in norm_A_steps(xt, npart, nsub, bufs):
                st_()

        def norm_A_steps(xt, npart, nsub, bufs):
            (junk, ssq, rstd, xnb, hT, ptr, b_xt, b_st, b_xnb, b_hT, b_ptr) = bufs
            steps = []
            for s in range(nsub):
                steps.append(lambda s=s: p.op("act", "activation", out=junk[:npart, :], in_=xt[:npart, s, :], func=AF.Square,
                                              accum_out=ssq[:npart, s:s + 1], reads=[b_xt], writes=[b_st]))

            def rs():
                if getattr(K, "rs_mode", "pow") == "sqrt":
                    p.op("act", "activation", out=rstd[:npart, :nsub], in_=ssq[:npart, :nsub], func=AF.Sqrt, scale=1.0 / D,
                         bias=epsT[:npart, 0:1], reads=[b_st, B_const], writes=[b_st])
                    p.op("dve", "reciprocal", out=rstd[:npart, :nsub], in_=rstd[:npart, :nsub], reads=[b_st], writes=[b_st])
                    return
                p.op("dve", "tensor_scalar", out=rstd[:npart, :nsub], in0=ssq[:npart, :nsub], scalar1=1.0 / D, scalar2=EPS,
                     op0=ALU.mult, op1=ALU.add, reads=[b_st], writes=[b_st])
                p.op("pool", "tensor_tensor", out=rstd[:npart, :nsub], in0=rstd[:npart, :nsub], in1=mhalf[:npart, :nsub], op=ALU.pow,
                     reads=[b_st, B_const], writes=[b_st])
            steps.append(rs)
            for s in range(nsub):
                if s % 2 == 0:
                    steps.append(lambda s=s: p.op("dve", "tensor_scalar", out=xnb[:npart, s, :], in0=xt[:npart, s, :],
                                                  scalar1=rstd[:npart, s:s + 1], scalar2=None, op0=ALU.mult, reads=[b_xt, b_st], writes=[b_xnb], nowaw=True))
                else:
                    steps.append(lambda s=s: p.op("act", "activation", out=xnb[:npart, s, :], in_=xt[:npart, s, :], func=AF.Copy,
                                                  scale=rstd[:npart, s:s + 1], reads=[b_xt, b_st], writes=[b_xnb], nowaw=True))
            return steps

        def norm_B(npart, nsub, gs, sh_j, t, bufs):
            for st_ in norm_B_steps(npart, nsub, gs, sh_j, t, bufs):
                st_()

        def norm_B_steps(npart, nsub, gs, sh_j, t, bufs):
            return [(lambda kc=kc: norm_B_kc(kc, npart, nsub, gs, sh_j, t, bufs)) for kc in range(8)]

        def norm_B_kc(kc, npart, nsub, gs, sh_j, t, bufs):
            (junk, ssq, rstd, xnb, hT, ptr, b_xt, b_st, b_xnb, b_hT, b_ptr) = bufs
            if True:
                pt = ptr[kc % 2]
                for s in range(nsub):
                    p.op("pe", "transpose", out=pt[:, s, :npart], in_=xnb[:npart, s, kc * 128:(kc + 1) * 128],
                         identity=identb[:npart, :npart], reads=[b_xnb, B_const], writes=[b_ptr[kc % 2]], inc=(s == nsub - 1))
                if nsub == 8 and kc % 2 == 1:
                    p.op("dve", "tensor_scalar", out=hT[:, kc, :nsub, :npart], in0=pt[:, :nsub, :npart], scalar1=gs[:, kc, t:t + 1],
                         scalar2=shcol(sh_j, kc, t), op0=ALU.mult, op1=ALU.add, reads=[b_ptr[kc % 2], B_mods], writes=[b_hT], nowaw=True)
                else:
                    p.op("act", "activation", out=hT[:, kc, :nsub, :npart], in_=pt[:, :nsub, :npart], func=AF.Identity,
                         scale=gs[:, kc, t:t + 1], bias=shcol(sh_j, kc, t), reads=[b_ptr[kc % 2], B_mods], writes=[b_hT], nowaw=True)

        def post_norm_add(py, b_py, xrow, b_x, G, b_G, tmp, b_tmp, sty, b_sty, junk, b_junk, s, add_engs=("pool", "pool")):
            p.op("act", "activation", out=junk[:, :], in_=py[:, :], func=AF.Square, accum_out=sty[:, s:s + 1],
                 reads=[b_py], writes=[b_sty, b_junk])
            if getattr(K, "rs_mode", "pow") == "sqrt":
                p.op("act", "activation", out=sty[:, 8 + s:9 + s], in_=sty[:, s:s + 1], func=AF.Sqrt, scale=1.0 / D,
                     bias=epsT[:, 0:1], reads=[b_sty, B_const], writes=[b_sty])
                p.op("dve", "reciprocal", out=sty[:, 8 + s:9 + s], in_=sty[:, 8 + s:9 + s], reads=[b_sty], writes=[b_sty])
            else:
                p.op("dve", "tensor_scalar", out=sty[:, 8 + s:9 + s], in0=sty[:, s:s + 1], scalar1=1.0 / D, scalar2=EPS,
                     op0=ALU.mult, op1=ALU.add, reads=[b_sty], writes=[b_sty])
                p.op("pool", "tensor_tensor", out=sty[:, 8 + s:9 + s], in0=sty[:, 8 + s:9 + s], in1=mhalf[:, 0:1], op=ALU.pow,
                     reads=[b_sty, B_const], writes=[b_sty])
            for h in range(2):
                sl = slice(h * 512, (h + 1) * 512)
                p.op("dve", "scalar_tensor_tensor", out=tmp[h][:, :], in0=py[:, sl], scalar=sty[:, 8 + s:9 + s], in1=G[:, sl],
                     op0=ALU.mult, op1=ALU.mult, reads=[b_py, b_sty, b_G], writes=[b_tmp[h]])
            for h in range(2):
                sl = slice(h * 512, (h + 1) * 512)
                p.op(add_engs[h], "tensor_tensor", out=xrow[:, sl], in0=xrow[:, sl], in1=tmp[h][:, :], op=ALU.add,
                     reads=[b_tmp[h], b_x], writes=[b_x])

        def ffn_weight_dmas(wgu_d, wd_d, kcs, do_wd, deferred=False, split=False):
            A = Arena(ARENA0, SB_TOP)
            wg = A.al("wgp", [128, 8, 2 * FF], BF16)
            wd = A.al("wdp", [128, FC, D], BF16)
            bb = Buf("ffnw_pre")
            bb_wd = Buf("ffnw_pre_wd") if split else bb
            todo = []
            for kc in kcs:
                todo.append(lambda kc=kc: p.dma("pool", wg[:, kc, :], wgu_d[kc * 128:(kc + 1) * 128, :], writes=[bb]))
            if do_wd:
                wdv = wd_d.rearrange("(fc p) n -> p fc n", p=128)
                for h in range(2):
                    todo.append(lambda h=h: p.dma("pool", wd[:, h * 11:(h + 1) * 11, :], wdv[:, h * 11:(h + 1) * 11, :], writes=[bb_wd]))
            if deferred:
                return bb, todo
            for f in todo:
                f()
            return (bb, bb_wd) if split else bb

        def phase_ffn(tag, wgu_d, wd_d, gs, sh_j, gc, tiles, pre_kcs=(), pre_wd=False, pre_buf=None, pre_buf_wd=None):
            K.rs_mode = "pow"
            A = Arena(ARENA0, SB_TOP)
            wg = A.al("wg", [128, 8, 2 * FF], BF16)
            wd = A.al("wd", [128, FC, D], BF16)
            xts = [A.al("xt", [128, 2, D], F32) for _ in range(3)]
            xnb = A.al("xnb", [128, 2, D], BF16)
            hTs = [A.al("hT", [128, 8, 2, 128], BF16) for _ in range(2)]
            aT = A.al("aT", [128, FC, 256], BF16)
            sg = [A.al("sg", [128, 512], F32) for _ in range(2)]
            tmp = [A.al("tmp", [128, 512], F32) for _ in range(2)]
            dg = [A.al("dg", [128, 128], F32) for _ in range(2)]
            junk = A.al("junk", [128, D], BF16)
            ssq = A.al("ssq", [128, 8], F32)
            rstd = A.al("rstd", [128, 8], F32)
            sty = A.al("sty", [128, 16], F32)
            Gt = A.al("G", [128, D], F32)
            with contextlib.ExitStack() as ps:
                b_Gt = Buf("G")
                bdg = [Buf("dg0"), Buf("dg1")]
                ptr = [ps.enter_context(nc.psum_tensor("ptr%s%d" % (tag, i), [128, 8, 128], BF16)) for i in range(2)]
                if os.environ.get("PGU", "0") == "1":
                    pgu = [ps.enter_context(nc.psum_tensor("pgu%s%d" % (tag, i), [128, 2, 256], F32)) for i in range(2)]
                    pys = [ps.enter_context(nc.psum_tensor("py%s%d" % (tag, i), [128, D], F32)) for i in range(2)]
                elif os.environ.get("PGU", "0") == "2":
                    pgu = [ps.enter_context(nc.psum_tensor("pgu%s%d" % (tag, i), [128, 2, 256], F32)) for i in range(2)]
                    pys = [ps.enter_context(nc.psum_tensor("py%s%d" % (tag, 0), [128, D], F32))] * 2
                else:
                    pgu = [ps.enter_context(nc.psum_tensor("pgu%s%d" % (tag, i), [128, 2, 512], F32)) for i in range(2)]
                    pys = [ps.enter_context(nc.psum_tensor("py%s%d" % (tag, 0), [128, D], F32))] * 2
                b_wg = [(pre_buf if (pre_buf is not None and k in pre_kcs) else Buf("wg%d" % k)) for k in range(8)]
                b_wd = [(pre_buf if (pre_buf is not None and pre_wd) else Buf("wd%d" % k)) for k in range(2)]
                if pre_buf_wd is not None:
                    b_wd = [pre_buf_wd, pre_buf_wd]
                b_xt = [Buf("xt0"), Buf("xt1"), Buf("xt2")]
                b_st, b_xnb, b_aT, b_sty, b_junk = (Buf(n) for n in ("st", "xnb", "aT", "sty", "junk"))
                b_hTs = [Buf("hT0"), Buf("hT1")]
                b_pys = [Buf("py0"), Buf("py1")]
                if os.environ.get("PGU", "0") != "1" or os.environ.get("PYSER", "0") == "1":
                    b_pys = [b_pys[0], b_pys[0]]
                b_pgu = [Buf("pgu0"), Buf("pgu1")]
                b_ptr = [Buf("ptr0"), Buf("ptr1")]
                b_sg = [Buf("sg0"), Buf("sg1")]
                b_tmp = [Buf("tmp0"), Buf("tmp1")]
                cur_t = [None]
                for kc in range(8):
                    if kc not in pre_kcs:
                        p.dma("pool", wg[:, kc, :], wgu_d[kc * 128:(kc + 1) * 128, :], writes=[b_wg[kc]])
                wdv = wd_d.rearrange("(fc p) n -> p fc n", p=128)
                for h in range(2):
                    if not pre_wd:
                        p.dma("pool", wd[:, h * 11:(h + 1) * 11, :], wdv[:, h * 11:(h + 1) * 11, :], writes=[b_wd[h]])

                def load(i):
                    src, _, nsub, _ = tiles[i]
                    p.dma("sp", xts[i % 3][:, :nsub, :], src.rearrange("(s p) d -> p s d", p=128), writes=[b_xt[i % 3]])
                def nbufs(i):
                    return (junk, ssq, rstd, xnb, hTs[i % 2], ptr, b_xt[i % 3], b_st, b_xnb, b_hTs[i % 2], b_ptr)

                def pe_up(i, fc):
                    N = tiles[i][2] * 128
                    hTf = hTs[i % 2][:].rearrange("p k s c -> p k (s c)")
                    pp = pgu[fc % 2]
                    for which in range(2):
                        c0 = which * FF + fc * 128
                        for kc in range(8):
                            p.op("pe", "matmul", pp[:, which, :N], lhsT=wg[:, kc, c0:c0 + 128], rhs=hTf[:, kc, :N],
                                 start=(kc == 0), stop=(kc == 7), reads=[b_wg[kc], b_hTs[i % 2]], writes=[b_pgu[fc % 2]],
                                 inc=(which == 1 and kc == 7))

                def evac_up(i, fc):
                    N = tiles[i][2] * 128
                    pp = pgu[fc % 2]
                    p.op("act", "activation", out=sg[fc % 2][:, :N], in_=pp[:, 0, :N], func=AF.Silu,
                         reads=[b_pgu[fc % 2]], writes=[b_sg[fc % 2]])
                    p.op("dve", "tensor_tensor", out=aT[:, fc, :N], in0=sg[fc % 2][:, :N], in1=pp[:, 1, :N], op=ALU.mult,
                         reads=[b_sg[fc % 2], b_pgu[fc % 2]], writes=[b_aT])

                def down(i, s):
                    py = pys[0]
                    for h in range(2):
                        for fc in range(FC):
                            p.op("pe", "matmul", py[:, h * 512:(h + 1) * 512], lhsT=aT[:, fc, s * 128:(s + 1) * 128],
                                 rhs=wd[:, fc, h * 512:(h + 1) * 512], start=(fc == 0), stop=(fc == FC - 1),
                                 reads=[b_aT, b_wd[fc // 11]], writes=[b_pys[0]], inc=(fc == FC - 1))
                load(0)
                norm_T(xts[0], 128, tiles[0][2], gs, sh_j, tiles[0][3], nbufs(0))
                for fc in range(2):
                    pe_up(0, fc)
                    evac_up(0, fc)
                nT = len(tiles)
                for i, (src, dst, nsub, t) in enumerate(tiles):
                    xt = xts[i % 3]
                    bx = b_xt[i % 3]
                    stepsA, stepsB = [], []
                    if i + 1 < nT:
                        load(i + 1)
                        stepsA = norm_A_steps(xts[(i + 1) % 3], 128, tiles[i + 1][2], nbufs(i + 1))
                        stepsB = norm_B_steps(128, tiles[i + 1][2], gs, sh_j, tiles[i + 1][3], nbufs(i + 1))
                    if cur_t[0] != t:
                        make_row(Gt, lambda kc, t=t: gc[:, kc, t:t + 1], b_Gt, pys[0], b_pys[0], dg, bdg)
                        cur_t[0] = t
                    for fc in range(2, FC):
                        pe_up(i, fc)
                        evac_up(i, fc)
                        if fc % 2 == 0 and fc <= 10 and stepsA:
                            stepsA.pop(0)()
                        if fc >= 13:
                            while stepsA:
                                stepsA.pop(0)()
                            if stepsB:
                                stepsB.pop(0)()
                    while stepsA:
                        stepsA.pop(0)()
                    while stepsB:
                        stepsB.pop(0)()
                    down(i, 0)
                    post_norm_add(pys[0], b_pys[0], xt[:, 0, :], bx, Gt, b_Gt, tmp, b_tmp, sty, b_sty, junk, b_junk, 0)
                    if i + 1 < nT:
                        pe_up(i + 1, 0)
                        pe_up(i + 1, 1)
                    down(i, 1)
                    if i + 1 < nT:
                        evac_up(i + 1, 0)
                        evac_up(i + 1, 1)
                    post_norm_add(pys[0], b_pys[0], xt[:, 1, :], bx, Gt, b_Gt, tmp, b_tmp, sty, b_sty, junk, b_junk, 1)
                    p.dma("pool", dst.rearrange("(s p) d -> p s d", p=128), xt[:, :nsub, :], reads=[bx], key=bx)
                p.barrier()
                p.emit()

        MA = Arena(OFF_MATS, OFF_YTM)
        MBre = MA.al("MBre", [128, NG, 128], BF16)
        MBim = MA.al("MBim", [128, NG, 128], BF16)
        QAre = MA.al("QAre", [128, NG, 128], BF16)
        QAim = MA.al("QAim", [128, NG, 128], BF16)
        Tz = MA.al("Tz", [128, NG, 128], BF16)
        rcol = MA.al("rcol", [128, NG], F32)
        psi = MA.al("psi", [128, NG], F32)
        cidx = MA.al("cidx", [128, 544], F32)
        B_mats = Buf("mats")
        Xx = sb("Xx", [128, NG, 512], BF16, OFF_YMIX)
        Xc = sb("Xc", [128, NG, 32], BF16, OFF_YMIX + 32768)
        XFREE = OFF_YMIX + 32768 + 2048
        Ytm = sb("Ytm", [128, 4, 8, 512], BF16, OFF_YTM)
        ymix = sb("ymix", [128, 8, L], BF16, OFF_YMIX)
        bgs = sb("bgs", [128, 4, L], BF16, OFF_BGS)

        def phase_s5setup():
            A = Arena(ARENA0, OFF_MATS)
            Bg = Arena(OFF_YTM, SB_TOP)
            s5a = A.al("s5a", [128, 3, NG], F32)
            s5b = A.al("s5b", [128, 4, NG, 16], F32)
            expo = A.al("expo", [128, 26], F32)
            mask = A.al("mask", [128, 2, 128], F32)
            dcol = A.al("dcol", [128, NG], F32)
            sm = {n: A.al(n, [128, NG], F32) for n in ("dt", "ar", "th", "k", "nr", "den", "t1", "t2", "qre", "qim")}
            big = {n: A.al(n, [128, NG, 26], F32) for n in ("AR", "MAG", "T", "T2", "Kk", "SIN", "COS", "PWre", "PWim")}
            Bb = {n: A.al(n, [128, NG, 16], F32) for n in ("Bbre", "Bbim", "b1", "b2")}
            P_re = Bg.al("P_re", [128, NG, 128], F32)
            P_im = Bg.al("P_im", [128, NG, 128], F32)
            Q8re = Bg.al("Q8re", [128, NG, 128], F32)
            nQ8im = Bg.al("nQ8im", [128, NG, 128], F32)
            tA = Bg.al("tA", [128, NG, 128], F32)
            tB = Bg.al("tB", [128, NG, 128], F32)
            rm = A.al("rm", [128, 2], F32)
            tzt = [A.al("tzt%d" % i, [128, 128], F32) for i in range(2)]
            tzu = [A.al("tzu%d" % i, [128, 128], F32) for i in range(2)]
            b_tzu = [Buf("tzu0"), Buf("tzu1")]
            b = Buf("s5s")
            bP, bQ, btA, btB = Buf("P"), Buf("Q"), Buf("tA"), Buf("tB")
            b_in = Buf("s5in")
            with contextlib.ExitStack() as ps:
                pT = [ps.enter_context(nc.psum_tensor("pT%d" % i, [128, 4, 128], F32)) for i in range(2)]
                ptz = [[ps.enter_context(nc.psum_tensor("ptz%d%d" % (i, j), [128, 2, 128], F32)) for j in range(2)] for i in range(2)]
                b_pT = [Buf("pT0"), Buf("pT1")]
                b_ptz = [Buf("ptz0"), Buf("ptz1")]
                b_tzt = [Buf("tzt0"), Buf("tzt1")]
                try:
                    cst = I.cst
                    p.dma("sp", s5a[:].rearrange("p a g -> p (a g)"), I.s5a, writes=[b_in])
                    p.dma("sp", s5b[:].rearrange("p a g h -> p (a g h)"), I.s5b, writes=[b_in])
                    p.dma("sp", mask[:].rearrange("p a n -> p (a n)"), cst[:, 128:384], writes=[b_in])
                    p.dma("sp", expo[:], cst[:, 384:410], writes=[b_in])
                    p.dma("sp", cidx[:], cst[:, 410:954], writes=[B_mats])
                    p.dma("sp", dcol[:], I.dcol, writes=[b_in])
                    lre, lim, ldt = s5a[:, 0, :], s5a[:, 1, :], s5a[:, 2, :]

                    def tt(out, a, bb, op, eng="dve"):
                        p.op(eng, "tensor_tensor", out=out, in0=a, in1=bb, op=op, reads=[b, b_in], writes=[b])

                    def ts(out, a, s1, op0, s2=None, op1=None):
                        kw = dict(out=out, in0=a, scalar1=s1, scalar2=s2, op0=op0)
                        if op1 is not None:
                            kw["op1"] = op1
                        p.op("dve", "tensor_scalar", reads=[b, b_in], writes=[b], **kw)

                    def rnd_frac(dst, src, k):
                        ts(k, src, MAGIC, ALU.add)
                        ts(k, k, -MAGIC, ALU.add)
                        tt(dst, src, k, ALU.subtract)
                    p.op("act", "activation", out=sm["dt"][:], in_=ldt, func=AF.Exp, reads=[b_in], writes=[b])
                    tt(sm["ar"][:], lre, sm["dt"][:], ALU.mult)
                    p.op("dve", "scalar_tensor_tensor", out=sm["th"][:], in0=lim, scalar=1.0 / TWO_PI, in1=sm["dt"][:],
                         op0=ALU.mult, op1=ALU.mult, reads=[b, b_in], writes=[b])
                    rnd_frac(sm["th"][:], sm["th"][:], sm["k"][:])
                    _cut(1)
                    e_b = _mk(expo[:, 0:1], [[0, NG], [1, 26]])

                    def gb(t):
                        return _mk(t[:, 0:1], [[1, NG], [0, 26]])
                    tt(big["AR"][:], gb(sm["ar"]), e_b, ALU.mult)
                    p.op("act", "activation", out=big["MAG"][:], in_=big["AR"][:], func=AF.Exp, reads=[b], writes=[b])
                    tt(big["T"][:], gb(sm["th"]), e_b, ALU.mult)
                    rnd_frac(big["SIN"][:], big["T"][:], big["Kk"][:])
                    ts(big["T2"][:], big["T"][:], 0.25, ALU.add)
                    rnd_frac(big["COS"][:], big["T2"][:], big["Kk"][:])
                    p.op("dve", "tensor_copy", out=psi[:], in_=big["SIN"][:, :, 24], reads=[b], writes=[B_mats])
                    p.op("dve", "tensor_copy", out=rcol[:], in_=big["MAG"][:, :, 24], reads=[b], writes=[B_mats])
                    p.op("act", "activation", out=big["SIN"][:], in_=big["SIN"][:], func=AF.Sin, scale=TWO_PI, reads=[b], writes=[b])
                    p.op("act", "activation", out=big["COS"][:], in_=big["COS"][:], func=AF.Sin, scale=TWO_PI, reads=[b], writes=[b])
                    tt(big["PWre"][:], big["MAG"][:], big["COS"][:], ALU.mult)
                    tt(big["PWim"][:], big["MAG"][:], big["SIN"][:], ALU.mult)
                    _cut(2)
                    a_re, a_im = big["PWre"][:, :, 25], big["PWim"][:, :, 25]
                    ts(sm["nr"][:], a_re, -1.0, ALU.add)
                    tt(sm["t1"][:], lre, lre, ALU.mult)
                    tt(sm["t2"][:], lim, lim, ALU.mult)
                    tt(sm["den"][:], sm["t1"][:], sm["t2"][:], ALU.add)
                    p.op("dve", "reciprocal", out=sm["den"][:], in_=sm["den"][:], reads=[b], writes=[b])
                    tt(sm["t1"][:], sm["nr"][:], lre, ALU.mult)
                    tt(sm["t2"][:], a_im, lim, ALU.mult)
                    tt(sm["qre"][:], sm["t1"][:], sm["t2"][:], ALU.add)
                    tt(sm["qre"][:], sm["qre"][:], sm["den"][:], ALU.mult)
                    tt(sm["t1"][:], a_im, lre, ALU.mult)
                    tt(sm["t2"][:], sm["nr"][:], lim, ALU.mult)
                    tt(sm["qim"][:], sm["t1"][:], sm["t2"][:], ALU.subtract)
                    tt(sm["qim"][:], sm["qim"][:], sm["den"][:], ALU.mult)

                    def qb(t):
                        return _mk(t[:, 0:1], [[1, NG], [0, 16]])
                    Bre, Bim, Cre, Cim = (s5b[:, i] for i in range(4))
                    tt(Bb["b1"][:], qb(sm["qre"]), Bre, ALU.mult)
                    tt(Bb["b2"][:], qb(sm["qim"]), Bim, ALU.mult)
                    tt(Bb["Bbre"][:], Bb["b1"][:], Bb["b2"][:], ALU.subtract)
                    tt(Bb["b1"][:], qb(sm["qre"]), Bim, ALU.mult)
                    tt(Bb["b2"][:], qb(sm["qim"]), Bre, ALU.mult)
                    tt(Bb["Bbim"][:], Bb["b1"][:], Bb["b2"][:], ALU.add)

                    def pw(t, c0):
                        return _mk(t[:, 0, c0:c0 + 1], [[26, NG], [1, 8], [0, 16]])

                    def hb(ap3):
                        return _mk(ap3[:, 0, 0:1], [[16, NG], [0, 8], [1, 16]])

                    def v4(t):
                        return t[:].rearrange("p g (i h) -> p g i h", h=16)

                    def cprod(dst_re, dst_im, pre, pim, xre, xim, neg_im=False, engs=("dve", "dve")):
                        e0, e1 = engs
                        p.op(e0, "tensor_tensor", out=v4(tA), in0=pre, in1=xre, op=ALU.mult, reads=[b, b_in], writes=[btA])
                        p.op(e1, "tensor_tensor", out=v4(tB), in0=pim, in1=xim, op=ALU.mult, reads=[b, b_in], writes=[btB])
                        p.op(e0, "tensor_tensor", out=dst_re[0], in0=v4(tA), in1=v4(tB), op=ALU.subtract, reads=[btA, btB], writes=[dst_re[1]])
                        p.op(e0, "tensor_tensor", out=v4(tA), in0=pre, in1=xim, op=ALU.mult, reads=[b, b_in], writes=[btA])
                        p.op(e1, "tensor_tensor", out=v4(tB), in0=pim, in1=xre, op=ALU.mult, reads=[b, b_in], writes=[btB])
                        if neg_im:
                            p.op("dve", "scalar_tensor_tensor", out=dst_im[0], in0=v4(tA), scalar=-1.0, in1=v4(tB),
                                 op0=ALU.mult, op1=ALU.subtract, reads=[btA, btB], writes=[dst_im[1]])
                        else:
                            p.op(e0, "tensor_tensor", out=dst_im[0], in0=v4(tA), in1=v4(tB), op=ALU.add, reads=[btA, btB], writes=[dst_im[1]])
                    _cut(3)
                    PWre, PWim = big["PWre"], big["PWim"]
                    cprod((v4(QAre), B_mats), (v4(QAim), B_mats), pw(PWre, 8), pw(PWim, 8), hb(Cre), hb(Cim), neg_im=True)
                    cprod((v4(P_re), bP), (v4(P_im), bP), pw(PWre, 0), pw(PWim, 0), hb(Bb["Bbre"][:]), hb(Bb["Bbim"][:]))
                    k = 0
                    for src, dst in ((P_re, MBre), (P_im, MBim)):
                        for g4 in range(8):
                            for gg in range(4):
                                p.op("pe", "matmul", pT[k % 2][:, gg, :], lhsT=src[:, g4 * 4 + gg, :], rhs=ident[:], start=True, stop=True,
                                     reads=[bP, B_const], writes=[b_pT[k % 2]], inc=(gg == 3))
                            p.op("act", "copy",
                                 out=dst[:, g4 * 4:(g4 + 1) * 4, :], in_=pT[k % 2][:], reads=[b_pT[k % 2]], writes=[B_mats])
                            k += 1
                    cprod((v4(Q8re), bQ), (v4(nQ8im), bQ), pw(PWre, int(os.environ.get('PWC','16'))), pw(PWim, int(os.environ.get('PWC','16'))), hb(Cre), hb(Cim), neg_im=(os.environ.get('NEG','1')=='1'))
                    _cut(5)
                    p.op("dve", "tensor_scalar", out=rm[:, 0:1], in0=expo[:, 0:1], scalar1=1.0 / 7.0, scalar2=None, op0=ALU.mult, reads=[b_in], writes=[b])
                    p.op("dve", "tensor_scalar", out=rm[:, 1:2], in0=expo[:, 7:8], scalar1=1.0 / 7.0, scalar2=None, op0=ALU.mult, reads=[b_in], writes=[b])
                    for dst, src, col in ((tA, P_re, 0), (tB, P_im, 0), (P_re, P_re, 1), (P_im, P_im, 1)):
                        p.op("act", "activation", out=dst[:], in_=src[:], func=AF.Copy, scale=rm[:, col:col + 1],
                             reads=[bP, b, B_mats], writes=[bP])
                    _cut(6)
                    for g2 in range(16):
                        if g2 == int(os.environ.get('G2STOP', '99')):
                            raise _Stop()
                        for gg in range(2):
                            g = g2 * 2 + gg
                            for d, (sre, sim) in enumerate(((tA, tB), (P_re, P_im))):
                                p.op("pe", "matmul", ptz[d][g2 % 2][:, gg, :], lhsT=sre[:, g, :], rhs=Q8re[:, g, :], start=True, stop=False,
                                     reads=[bP, bQ], writes=[b_ptz[g2 % 2]], inc=False)
                                p.op("pe", "matmul", ptz[d][g2 % 2][:, gg, :], lhsT=sim[:, g, :], rhs=nQ8im[:, g, :], start=False, stop=True,
                                     reads=[bP, bQ], writes=[b_ptz[g2 % 2]], inc=True)
                        for gg in range(2 if os.environ.get('NODVE', '0') == '0' else 0):
                            g = g2 * 2 + gg
                            tz = tzt[g % 2]
                            btz = b_tzt[g % 2]
                            p.op("dve", "tensor_tensor", out=tz[:], in0=ptz[0][g2 % 2][:, gg, :], in1=mask[:, 0, :], op=ALU.mult,
                                 reads=[b_ptz[g2 % 2], b_in], writes=[btz])
                            p.op("dve", "scalar_tensor_tensor", out=tz[:], in0=ident[:], scalar=dcol[:, g:g + 1], in1=tz[:],
                                 op0=ALU.mult, op1=ALU.add, reads=[btz, b_in, B_const], writes=[btz])
                            p.op("dve", "tensor_tensor", out=tzu[g % 2][:], in0=ptz[1][g2 % 2][:, gg, :], in1=mask[:, 1, :], op=ALU.mult,
                                 reads=[b_ptz[g2 % 2], b_in], writes=[b_tzu[g % 2]])
                            p.op("dve", "tensor_tensor", out=Tz[:, g, :], in0=tz[:], in1=tzu[g % 2][:], op=ALU.add,
                                 reads=[btz, b_tzu[g % 2]], writes=[B_mats])
                except _Stop:
                    pass
                for name, t in (("MBre", MBre), ("MBim", MBim), ("QAre", QAre), ("QAim", QAim), ("Tz", Tz)):
                    if name in dbg:
                        p.op("dve", "tensor_copy", out=Q8re[:], in_=t[:], reads=[B_mats, bQ], writes=[bQ])
                        p.dma("sp", dbg[name], Q8re[:].rearrange("p g n -> p (g n)"), reads=[bQ], key=Buf("k"))
                p.barrier()
                p.emit()

        def phase_m1():
            K.rs_mode = "sqrt"
            A = Arena(ARENA0, OFF_MATS)
            wu = A.al("wu", [128, 8, 512], BF16)
            hT = A.al("hT", [128, 8, 8, 128], BF16)
            junk = A.al("junk", [128, D], BF16)
            ssq = A.al("ssq", [128, 8], F32)
            rstd = A.al("rstd", [128, 8], F32)
            xts = [sb("xtm0", [128, 8, D], F32, OFF_YTM), A.al("xtm1", [128, 8, D], F32)]
            XA = Arena(XFREE, SB_TOP)
            xnb = XA.al("xnb", [128, 8, D], BF16)
            Utm = XA.al("Utm", [128, NG, 8, 16], BF16)
            with contextlib.ExitStack() as ps:
                ptr = [ps.enter_context(nc.psum_tensor("ptrm%d" % i, [128, 8, 128], BF16)) for i in range(2)]
                pU = [ps.enter_context(nc.psum_tensor("pU%d" % i, [128, 512], F32)) for i in range(2)]
                pX = [ps.enter_context(nc.psum_tensor("pX%d" % i, [128, 4, 128], BF16)) for i in range(2)]
                b_wu = Buf("wu")
                b_xt = [Buf("xt0"), Buf("xt1")]
                b_st, b_xnb, b_hT, b_U, b_X = (Buf(n) for n in ("st", "xnb", "hT", "Utm", "X"))
                b_ptr = [Buf("ptr0"), Buf("ptr1")]
                b_pU = [Buf("pU0"), Buf("pU1")]
                b_pX = [Buf("pX0"), Buf("pX1")]
                K.b_X = b_X
                for kc in range(8):
                    p.dma("pool", wu[:, kc, :], I.w_in[kc * 128:(kc + 1) * 128, 1536:2048], writes=[b_wu])
                tl = [(c1s, 32, Xc, 0, 1)] + [(x1s[i * 1024:(i + 1) * 1024, :], 128, Xx, i * 128, 0) for i in range(4)]

                def load(i):
                    src, npart = tl[i][0], tl[i][1]
                    p.dma("sp", xts[i % 2][:npart, :, :], src.rearrange("(c i) d -> c i d", i=8), writes=[b_xt[i % 2]])
                def mbufs(i):
                    return (junk, ssq, rstd, xnb, hT, ptr, b_xt[i % 2], b_st, b_xnb, b_hT, b_ptr)
                load(0)
                norm_T(xts[0], tl[0][1], 8, gsm, 3, tl[0][4], mbufs(0))
                for i, (src, npart, Xdst, col0, t) in enumerate(tl):
                    steps = []
                    if i + 1 < len(tl):
                        load(i + 1)
                        steps = norm_A_steps(xts[(i + 1) % 2], tl[i + 1][1], 8, mbufs(i + 1))
                    for ii in range(8):
                        for kc in range(8):
                            p.op("pe", "matmul", pU[ii % 2][:npart, :], lhsT=hT[:, kc, ii, :npart], rhs=wu[:, kc, :],
                                 start=(kc == 0), stop=(kc == 7), reads=[b_hT, b_wu], writes=[b_pU[ii % 2]], inc=(kc == 7))
                        if False:
                            p.op("act", "copy", out=Utm[:npart, :, ii, :], in_=pU[ii % 2][:npart, :].rearrange("p (g h) -> p g h", h=16),
                                 reads=[b_pU[ii % 2]], writes=[b_U], nowaw=True)
                        else:
                            p.op("dve", "tensor_copy", out=Utm[:npart, :, ii, :], in_=pU[ii % 2][:npart, :].rearrange("p (g h) -> p g h", h=16),
                                 reads=[b_pU[ii % 2]], writes=[b_U], nowaw=True)
                        for _ in range(3):
                            if steps:
                                steps.pop(0)()
                    while steps:
                        steps.pop(0)()
                    if i + 1 < len(tl):
                        norm_B(tl[i + 1][1], 8, gsm, 3, tl[i + 1][4], mbufs(i + 1))
                    for g4 in range(8):
                        px = pX[g4 % 2]
                        for gg in range(4):
                            g = g4 * 4 + gg
                            p.op("pe", "transpose", out=px[:, gg, :npart], in_=Utm[:npart, g, :, :].rearrange("p i h -> p (i h)"),
                                 identity=identb[:npart, :npart], reads=[b_U, B_const], writes=[b_pX[g4 % 2]], inc=(gg == 3))
                        if g4 % 4 == 0:
                            p.op("act", "copy", out=Xdst[:, g4 * 4:(g4 + 1) * 4, col0:col0 + npart], in_=px[:, :, :npart],
                                 reads=[b_pX[g4 % 2]], writes=[b_X], nowaw=True)
                        else:
                            p.op("dve", "tensor_copy", out=Xdst[:, g4 * 4:(g4 + 1) * 4, col0:col0 + npart], in_=px[:, :, :npart],
                                 reads=[b_pX[g4 % 2]], writes=[b_X], nowaw=True)
                if "Xx" in dbg:
                    p.barrier()
                    p.emit()
                    xf = sb("xf", [128, 16, 512], F32, OFF_YTM)
                    bxf = Buf("xf")
                    for h in range(2):
                        p.op("dve", "tensor_copy", out=xf[:], in_=Xx[:, h * 16:(h + 1) * 16, :], reads=[b_X, bxf], writes=[bxf])
                        p.dma("sp", dbg["Xx"][:, h * 8192:(h + 1) * 8192], xf[:].rearrange("p g c -> p (g c)"), reads=[bxf], key=bxf)
                    p.op("dve", "tensor_copy", out=xf[:, 0:2, :].rearrange("p a (b c) -> p (a b) c", b=16), in_=Xc[:], reads=[b_X, bxf], writes=[bxf])
                    p.dma("sp", dbg["Xc"], xf[:, 0:2, :].rearrange("p a c -> p (a c)"), reads=[bxf], key=bxf)
                p.barrier()
                p.emit()

        def phase_s5():
            A = Arena(ARENA0, OFF_MATS)
            NS = 544
            sets = []
            for i in range(2):
                d_ = {n: A.al(n + str(i), [128, NS], F32) for n in ("SUre", "SUim", "Gre", "Gim", "Zre", "Zim")}
                d_["HKre"] = A.al("HKre%d" % i, [128, 512], BF16)
                d_["HKim"] = A.al("HKim%d" % i, [128, 512], BF16)
                d_["b"] = {n: Buf(n) for n in ("SU", "Gre", "Gim", "Zre", "Zim", "HK")}
                sets.append(d_)
            phs = []
            for i in range(3):
                phs.append({"ts": A.al("ts%d" % i, [128, NS], F32), "tc": A.al("tc%d" % i, [128, NS], F32), "b": Buf("ph%d" % i)})
            sh = {n: A.al(n, [128, NS], F32) for n in ("tq", "k", "tq2", "k2", "m1", "m2", "m3", "m4")}
            b_sh = {n: Buf(n) for n in sh}
            with contextlib.ExitStack() as ps:
                pS = [[ps.enter_context(nc.psum_tensor("pS%d%d" % (i, j), [128, 512], F32)) for j in range(2)] for i in range(2)]
                pSc = [ps.enter_context(nc.psum_tensor("pSc%d" % i, [128, 64], F32)) for i in range(2)]
                pY = [ps.enter_context(nc.psum_tensor("pY%d" % i, [128, 4, 128], F32)) for i in range(2)]
                b_pS = [Buf("pS0"), Buf("pS1")]
                b_pY = [Buf("pY0"), Buf("pY1")]
                b_X = Buf("Xs5")
                b_Ytm = Buf("Ytm")

                def rev(ap, n):
                    return bass.AP(ap.tensor, ap.offset + (n - 1), [list(ap.ap[0]), [-1, n]])

                def stage_p(g):
                    P_ = phs[g % 3]
                    bp_ = P_["b"]
                    tq, kk, tq2, k2 = sh["tq"], sh["k"], sh["tq2"], sh["k2"]
                    Mb = _mk(magt[:, 0:1], [[0, NS]])
                    p.op("act", "activation", out=tq[:], in_=cidx[:], func=AF.Copy, scale=psi[:, g:g + 1], reads=[B_mats], writes=[b_sh["tq"]])
                    p.op("act", "activation", out=tq2[:], in_=cidx[:], func=AF.Identity, scale=psi[:, g:g + 1], bias=magt[:, 2:3],
                         reads=[B_mats, B_const], writes=[b_sh["tq2"]])
                    p.op("act", "activation", out=k2[:], in_=tq2[:], func=AF.Identity, bias=magt[:, 0:1], reads=[b_sh["tq2"], B_const], writes=[b_sh["k2"]])
                    p.op("act", "activation", out=k2[:], in_=k2[:], func=AF.Identity, bias=magt[:, 1:2], reads=[b_sh["k2"], B_const], writes=[b_sh["k2"]])
                    p.op("pool", "tensor_tensor", out=kk[:], in0=tq[:], in1=Mb, op=ALU.add, reads=[b_sh["tq"], B_const], writes=[b_sh["k"]])
                    p.op("pool", "tensor_tensor", out=kk[:], in0=kk[:], in1=Mb, op=ALU.subtract, reads=[b_sh["k"], B_const], writes=[b_sh["k"]])
                    p.op("pool", "tensor_tensor", out=P_["ts"][:], in0=tq[:], in1=kk[:], op=ALU.subtract, reads=[b_sh["tq"], b_sh["k"]], writes=[bp_])
                    p.op("pool", "tensor_tensor", out=P_["tc"][:], in0=tq2[:], in1=k2[:], op=ALU.subtract, reads=[b_sh["tq2"], b_sh["k2"]], writes=[bp_])
                    p.op("act", "activation", out=P_["ts"][:], in_=P_["ts"][:], func=AF.Sin, scale=TWO_PI, reads=[bp_], writes=[bp_])
                    p.op("act", "activation", out=P_["tc"][:], in_=P_["tc"][:], func=AF.Sin, scale=TWO_PI, reads=[bp_], writes=[bp_])

                def stage_a(g):
                    S = sets[g % 2]
                    bs = S["b"]
                    ps_re, ps_im = pS[g % 2]
                    psc = pSc[g % 2]
                    bp = b_pS[g % 2]
                    p.op("pe", "matmul", ps_re[:, :], lhsT=MBre[:, g, :], rhs=Xx[:, g, :], start=True, stop=True, reads=[B_mats, b_X], writes=[bp], inc=False)
                    p.op("pe", "matmul", ps_im[:, :], lhsT=MBim[:, g, :], rhs=Xx[:, g, :], start=True, stop=True, reads=[B_mats, b_X], writes=[bp], inc=False)
                    p.op("pe", "matmul", psc[:, 0:32], lhsT=MBre[:, g, :], rhs=Xc[:, g, :], start=True, stop=True, reads=[B_mats, b_X], writes=[bp], inc=False)
                    p.op("pe", "matmul", psc[:, 32:64], lhsT=MBim[:, g, :], rhs=Xc[:, g, :], start=True, stop=True, reads=[B_mats, b_X], writes=[bp], inc=True)
                    for nm, pss, c0 in (("SUre", ps_re, 0), ("SUim", ps_im, 32)):
                        SU = S[nm]
                        p.op("act", "copy", out=SU[0:64, 0:32], in_=psc[0:64, c0:c0 + 32], reads=[bp], writes=[bs["SU"]])
                        p.op("act", "copy", out=SU[0:64, 32:544], in_=pss[0:64, :], reads=[bp], writes=[bs["SU"]])
                        p.op("act", "copy", out=SU[64:128, 0:32], in_=rev(psc[64:128, c0:c0 + 32], 32), reads=[bp], writes=[bs["SU"]])
                        p.op("act", "copy", out=SU[64:128, 32:544], in_=rev(pss[64:128, :], 512), reads=[bp], writes=[bs["SU"]])

                def stage_b(g):
                    S = sets[g % 2]
                    bs = S["b"]
                    P_ = phs[g % 3]
                    sn, cs, bph = P_["ts"], P_["tc"], P_["b"]
                    m = [sh["m1"], sh["m2"], sh["m3"], sh["m4"]]
                    bm = [b_sh["m1"], b_sh["m2"], b_sh["m3"], b_sh["m4"]]

                    def mul(i, a, bsrc, bb, eng):
                        p.op(eng, "tensor_tensor", out=m[i][:], in0=a[:], in1=bb[:], op=ALU.mult, reads=[bsrc, bph], writes=[bm[i]])
                    mul(0, S["SUre"], bs["SU"], cs, "dve")
                    mul(1, S["SUim"], bs["SU"], sn, "dve")
                    mul(2, S["SUim"], bs["SU"], cs, "dve")
                    mul(3, S["SUre"], bs["SU"], sn, "dve")
                    p.op("dve", "tensor_tensor", out=S["Gre"][:], in0=m[0][:], in1=m[1][:], op=ALU.add, reads=[bm[0], bm[1]], writes=[bs["Gre"]])
                    p.op("dve", "tensor_tensor", out=S["Gim"][:], in0=m[2][:], in1=m[3][:], op=ALU.subtract, reads=[bm[2], bm[3]], writes=[bs["Gim"]])
                    rb = _mk(rcol[:, g:g + 1], [[0, NS]])
                    p.op("dve", "tensor_tensor_scan", out=S["Zre"][:], data0=rb, data1=S["Gre"][:], initial=0.0, op0=ALU.mult, op1=ALU.add,
                         reads=[bs["Gre"], B_mats], writes=[bs["Zre"]])
                    p.op("dve", "tensor_tensor_scan", out=S["Zim"][:], data0=rb, data1=S["Gim"][:], initial=0.0, op0=ALU.mult, op1=ALU.add,
                         reads=[bs["Gim"], B_mats], writes=[bs["Zim"]])
                    mul(0, S["Zre"], bs["Zre"], cs, "dve")
                    mul(1, S["Zim"], bs["Zim"], sn, "dve")
                    mul(2, S["Zim"], bs["Zim"], cs, "dve")
                    mul(3, S["Zre"], bs["Zre"], sn, "dve")
                    for nm, hn, ia, ib, op in (("HKre", "Gre", 0, 1, ALU.subtract), ("HKim", "Gim", 2, 3, ALU.add)):
                        HK = S[nm]
                        Hh = S[hn]
                        p.op("dve", "tensor_tensor", out=Hh[:], in0=m[ia][:], in1=m[ib][:], op=op, reads=[bm[ia], bm[ib]], writes=[bs[hn]])
                        p.op("act", "copy", out=HK[0:64, :], in_=Hh[0:64, 31:543], reads=[bs[hn]], writes=[bs["HK"]])
                        p.op("act", "copy", out=rev(HK[64:128, :], 512), in_=Hh[64:128, 31:543], reads=[bs[hn]], writes=[bs["HK"]])

                def stage_c(g):
                    S = sets[g % 2]
                    bs = S["b"]
                    py = pY[g % 2]
                    for cb in range(4):
                        cs_ = slice(cb * 128, (cb + 1) * 128)
                        p.op("pe", "matmul", py[:, cb, :], lhsT=Xx[:, g, cs_], rhs=Tz[:, g, :], start=True, stop=False,
                             reads=[b_X, B_mats], writes=[b_pY[g % 2]], inc=False)
                        p.op("pe", "matmul", py[:, cb, :], lhsT=S["HKre"][:, cs_], rhs=QAre[:, g, :], start=False, stop=False,
                             reads=[bs["HK"], B_mats], writes=[b_pY[g % 2]], inc=False)
                        p.op("pe", "matmul", py[:, cb, :], lhsT=S["HKim"][:, cs_], rhs=QAim[:, g, :], start=False, stop=True,
                             reads=[bs["HK"], B_mats], writes=[b_pY[g % 2]], inc=(cb == 3))
                    p.op("act", "copy", out=Ytm[:, :, :, g * 16:(g + 1) * 16], in_=py[:, :, :].rearrange("p c (j h) -> p c j h", h=16),
                         reads=[b_pY[g % 2]], writes=[b_Ytm])
                K.wc_pre = None
                if "m2a" in phases:
                    K.wc_pre = Buf("wc_pre")
                    wc_p = sb("wc_pre", [128, 8, 1536], BF16, XFREE)
                    for kc in range(8):
                        p.dma("pool", wc_p[:, kc, :], I.w_in[kc * 128:(kc + 1) * 128, 0:1536], writes=[K.wc_pre])
                stage_p(0)
                stage_p(1)
                stage_a(0)
                for g in range(NG):
                    if g + 2 < NG:
                        stage_p(g + 2)
                    if g + 1 < NG:
                        stage_a(g + 1)
                    stage_b(g)
                    stage_c(g)
                if "Ytm" in dbg:
                    XA = Arena(ARENA0, OFF_MATS)
                    p.barrier()
                    p.emit()
                    tmpf = XA.al("ytmf", [128, 8, 512], F32)
                    bt = Buf("ytmf")
                    for cb in range(4):
                        p.op("dve", "tensor_copy", out=tmpf[:], in_=Ytm[:, cb], reads=[b_Ytm, bt], writes=[bt])
                        p.dma("sp", dbg["Ytm"][cb * 128:(cb + 1) * 128, :], tmpf[:].rearrange("p j c -> p (j c)"), reads=[bt], key=bt)
                p.barrier(keep=([K.wc_pre] if K.wc_pre is not None else []))
                p.emit()

        def phase_m2a():
            K.rs_mode = "sqrt"
            A = Arena(ARENA0, OFF_BGS)
            wc_pre = getattr(K, "wc_pre", None)
            wc = sb("wc", [128, 8, 1536], BF16, XFREE) if wc_pre is not None else A.al("wc", [128, 8, 1536], BF16)
            xts = [A.al("xt", [128, 2, D], F32) for _ in range(3)]
            xnb = A.al("xnb", [128, 2, D], BF16)
            hTs = [A.al("hT", [128, 8, 2, 128], BF16) for _ in range(2)]
            cgs = [A.al("cgs", [128, 256], F32) for _ in range(2)]
            junk = A.al("junk", [128, D], BF16)
            ssq = A.al("ssq", [128, 8], F32)
            rstd = A.al("rstd", [128, 8], F32)
            with contextlib.ExitStack() as ps:
                ptr = [ps.enter_context(nc.psum_tensor("ptrc%d" % i, [128, 8, 128], BF16)) for i in range(2)]
                pq = [[ps.enter_context(nc.psum_tensor("pq%d%d" % (i, j), [128, 512], F32)) for j in range(3)] for i in range(2)]
                b_wc = [Buf("wc%d" % k) for k in range(8)]
                b_xt = [Buf("xt0"), Buf("xt1"), Buf("xt2")]
                b_st, b_xnb, b_z, b_bg = (Buf(n) for n in ("st", "xnb", "z", "bg"))
                b_hTs = [Buf("hT0"), Buf("hT1")]
                b_ptr = [Buf("ptr0"), Buf("ptr1")]
                b_pq = [[Buf("pq") for j in range(3)] for i in range(2)]
                b_cgs = [Buf("cgs0"), Buf("cgs1")]
                if wc_pre is not None:
                    b_wc = [wc_pre] * 8
                else:
                    for kc in range(8):
                        p.dma("pool", wc[:, kc, :], I.w_in[kc * 128:(kc + 1) * 128, 0:1536], writes=[b_wc[kc]])
                NT = 16

                def load(i):
                    p.dma("sp", xts[i % 3][:, :, :], x1s[i * 256:(i + 1) * 256, :].rearrange("(s p) d -> p s d", p=128), writes=[b_xt[i % 3]])

                def nbufs(i):
                    return (junk, ssq, rstd, xnb, hTs[i % 2], ptr, b_xt[i % 3], b_st, b_xnb, b_hTs[i % 2], b_ptr)
                load(0)
                norm_T(xts[0], 128, 2, gsm, 3, 0, nbufs(0))
                k = 0
                for i in range(NT):
                    stepsA, stepsB = [], []
                    if i + 1 < NT:
                        load(i + 1)
                        stepsA = norm_A_steps(xts[(i + 1) % 3], 128, 2, nbufs(i + 1))
                        stepsB = norm_B_steps(128, 2, gsm, 3, 0, nbufs(i + 1))
                    tok = slice(i * 256, (i + 1) * 256)
                    hTf = hTs[i % 2][:].rearrange("p k s c -> p k (s c)")
                    b_hT = b_hTs[i % 2]
                    for q in range(4):
                        pp = pq[k % 2]
                        bp = b_pq[k % 2]
                        for j, c0 in enumerate((512 + q * 128, 1024 + q * 128, q * 128)):
                            for kc in range(8):
                                p.op("pe", "matmul", pp[j][:, :256], lhsT=wc[:, kc, c0:c0 + 128], rhs=hTf[:, kc, :],
                                     start=(kc == 0), stop=(kc == 7), reads=[b_wc[kc], b_hT], writes=[bp[j]], inc=(kc == 7))
                            if q >= 1 and stepsB and not stepsA:
                                stepsB.pop(0)()
                        p.op("dve", "tensor_copy", out=cgs[k % 2][:, :], in_=pp[0][:, :256], reads=[bp[0]], writes=[b_cgs[k % 2]])
                        p.op("dve", "tensor_copy", out=bgs[:, q, tok], in_=pp[2][:, :256], reads=[bp[2]], writes=[b_bg])
                        p.op("dve", "tensor_tensor", out=ymix[:, q, tok], in0=cgs[k % 2][:, :], in1=pp[1][:, :256], op=ALU.mult,
                             reads=[b_cgs[k % 2], bp[1]], writes=[b_z])
                        k += 1
                        if q == 0:
                            while stepsA:
                                stepsA.pop(0)()
                    while stepsA:
                        stepsA.pop(0)()
                    while stepsB:
                        stepsB.pop(0)()
                p.barrier()
                p.emit()

        def phase_m2b():
            A = Arena(ARENA0, OFF_BGS)
            acc = [A.al("acc", [128, L], F32) for _ in range(2)]
            tsh = [A.al("tsh", [128, L], F32) for _ in range(2)]
            b_acc = [Buf("acc0"), Buf("acc1")]
            b_tsh = [Buf("tsh0"), Buf("tsh1")]
            b_z = [Buf("z%d" % q) for q in range(4)]
            b_bg = Buf("bg")
            for q in range(4):
                a = acc[q % 2]
                ba = b_acc[q % 2]
                z = ymix[:, q, :]
                w0, w1, w2 = (cwt[:, q, j:j + 1] for j in range(3))
                p.op("act", "activation", out=a[:], in_=z, func=AF.Copy, scale=w1, reads=[b_z[q], B_const], writes=[ba])
                if q < 2:
                    zf = tsh[q % 2]
                    p.op("act", "copy", out=zf[:], in_=z, reads=[b_z[q]], writes=[b_tsh[q % 2]])
                    zv = zf[:].rearrange("p (r c) -> p r c", c=64)
                    av = a[:].rearrange("p (r c) -> p r c", c=64)
                    lo_out, lo_in = av[:, :, 1:64], zv[:, :, 0:63]
                    hi_out, hi_in = av[:, :, 0:63], zv[:, :, 1:64]
                else:
                    lo_out, lo_in = a[:, 64:L], z[:, 0:L - 64]
                    hi_out, hi_in = a[:, 0:L - 64], z[:, 64:L]
                p.op("dve", "scalar_tensor_tensor", out=lo_out, in0=lo_in, scalar=w0, in1=lo_out, op0=ALU.mult, op1=ALU.add,
                     reads=[b_z[q], b_tsh[q % 2], ba, B_const], writes=[ba])
                p.op("dve", "scalar_tensor_tensor", out=hi_out, in0=hi_in, scalar=w2, in1=hi_out, op0=ALU.mult, op1=ALU.add,
                     reads=[b_z[q], b_tsh[q % 2], ba, B_const], writes=[ba])
                p.op("pool", "tensor_tensor", out=z, in0=bgs[:, q, :], in1=a[:], op=ALU.mult, reads=[b_bg, ba], writes=[b_z[q]])
            p.barrier()
            p.emit()

        def phase_m2c():
            A = Arena(ARENA0, OFF_YTM)
            ys = A.al("ys", [128, 4, L], BF16)
            pre_wo = ("m2d" in phases) and ("ffn2" in phases)
            if pre_wo:
                wo_off = ARENA0 + 78848
                wo_p = sb("wo_pre", [128, 8, D], BF16, wo_off)
            wgl = A.al("wgl", [128, 4, 512], BF16)
            sgl = [A.al("sgl", [128, 512], F32) for _ in range(2)]
            assert (not pre_wo) or A.ptr <= ARENA0 + 78848
            with contextlib.ExitStack() as ps:
                pt2 = [ps.enter_context(nc.psum_tensor("pt2%d" % i, [128, 8, 128], BF16)) for i in range(2)]
                pgl = [ps.enter_context(nc.psum_tensor("pgl%d" % i, [128, 512], F32)) for i in range(2)]
                b_pt2 = [Buf("pt20"), Buf("pt21")]
                b_pgl = [Buf("pgl0"), Buf("pgl1")]
                b_sgl = [Buf("sgl0"), Buf("sgl1")]
                b_ys, b_w, b_Y, b_o = Buf("ys"), Buf("wgl"), Buf("Ytm"), Buf("yssm")
                for m in range(4):
                    p.dma("pool", wgl[:, m, :], I.w_glu[m * 128:(m + 1) * 128, :], writes=[b_w])
                K.wo_pre = None
                if pre_wo:
                    K.wo_pre = Buf("wo_pre")
                    for kc in range(8):
                        p.dma("pool", wo_p[:, kc, :], I.w_out[kc * 128:(kc + 1) * 128, :], writes=[K.wo_pre])
                k = 0
                for cb in range(4):
                    for m in range(4):
                        pt = pt2[k % 2]
                        for j in range(8):
                            p.op("pe", "transpose", out=pt[:, j, :], in_=Ytm[:, cb, j, m * 128:(m + 1) * 128], identity=identb[:],
                                 reads=[b_Y, B_const], writes=[b_pt2[k % 2]], inc=(j == 7))
                        src = _mk(pt[:, 0, 0:1], [[1, 128], [128, 8]])
                        p.op("act", "activation", out=ys[:, m, cb * 1024:(cb + 1) * 1024].rearrange("p (c j) -> p c j", j=8), in_=src,
                             func=AF.Gelu_apprx_tanh, reads=[b_pt2[k % 2]], writes=[b_ys])
                        k += 1
                k = 0
                for tb in range(8):
                    tok = slice(tb * 512, (tb + 1) * 512)
                    for co in range(4):
                        pg_ = pgl[k % 2]
                        for m in range(4):
                            p.op("pe", "matmul", pg_[:, :], lhsT=wgl[:, m, co * 128:(co + 1) * 128], rhs=ys[:, m, tok],
                                 start=(m == 0), stop=(m == 3), reads=[b_w, b_ys], writes=[b_pgl[k % 2]], inc=(m == 3))
                        p.op("act", "activation", out=sgl[k % 2][:, :], in_=pg_[:, :], func=AF.Sigmoid, bias=bglut[:, co:co + 1],
                             reads=[b_pgl[k % 2], B_const], writes=[b_sgl[k % 2]])
                        p.op("dve", "tensor_tensor", out=ymix[:, 4 + co, tok], in0=ys[:, co, tok], in1=sgl[k % 2][:, :], op=ALU.mult,
                             reads=[b_ys, b_sgl[k % 2]], writes=[b_o])
                        k += 1
                if "ymix" in dbg:
                    p.barrier()
                    p.emit()
                    yf = A.al("yf", [128, 2, L], F32)
                    byf = Buf("yf")
                    for h in range(4):
                        p.op("dve", "tensor_copy", out=yf[:], in_=ymix[:, h * 2:(h + 1) * 2, :], reads=[byf], writes=[byf])
                        p.dma("sp", dbg["ymix"][:, h * 2 * L:(h + 1) * 2 * L], yf[:].rearrange("p a t -> p (a t)"), reads=[byf], key=byf)
                p.barrier(keep=([K.wo_pre] if K.wo_pre is not None else []))
                p.emit()

        def phase_m2d():
            K.rs_mode = "sqrt"
            pre = "ffn2" in phases
            A = Arena(ARENA0 + (78848 if pre else 0), OFF_YMIX)
            wo = A.al("wo", [128, 8, D], BF16)
            xts = [A.al("xt", [128, 2, D], F32) for _ in range(3)]
            tmp = [A.al("tmp", [128, 512], F32) for _ in range(2)]
            dg = [A.al("dg", [128, 128], F32) for _ in range(2)]
            junk = A.al("junk", [128, D], BF16)
            sty = A.al("sty", [128, 16], F32)
            Gt = A.al("G", [128, D], F32)
            with contextlib.ExitStack() as ps:
                pys = [ps.enter_context(nc.psum_tensor("pyo%d" % i, [128, D], F32)) for i in range(3)]
                b_wo = Buf("wo")
                b_xt = [Buf("xt0"), Buf("xt1"), Buf("xt2")]
                b_py = [Buf("py0"), Buf("py1"), Buf("py2")]
                b_tmp = [Buf("tmp0"), Buf("tmp1")]
                bdg = [Buf("dg0"), Buf("dg1")]
                b_sty, b_junk, b_Gt, b_ym = Buf("sty"), Buf("junk"), Buf("G"), Buf("ymix")
                b_sty2 = [Buf("sty0"), Buf("sty1")]
                if getattr(K, "wo_pre", None) is not None:
                    b_wo = K.wo_pre
                else:
                    for kc in range(8):
                        p.dma("pool", wo[:, kc, :], I.w_out[kc * 128:(kc + 1) * 128, :], writes=[b_wo])
                K.ffn2_pre = None
                pre_dmas = []
                if pre:
                    K.ffn2_pre, pre_dmas = ffn_weight_dmas(I.f2_wgu, I.f2_wd, range(7), False, deferred=True)
                make_row(Gt, lambda kc: gcm[:, kc, 0:1], b_Gt, pys[0], b_py[0], dg, bdg)
                NT = 16

                def load(i):
                    p.dma("sp", xts[i % 3][:, :, :], x1s[i * 256:(i + 1) * 256, :].rearrange("(s p) d -> p s d", p=128), writes=[b_xt[i % 3]])
                load(0)
                k = 0
                for i in range(NT):
                    if i + 1 < NT:
                        load(i + 1)
                    xt = xts[i % 3]
                    bx = b_xt[i % 3]
                    for s_ in range(2):
                        t0 = i * 256 + s_ * 128
                        py = pys[k % 3]
                        for h in range(2):
                            for m in range(8):
                                p.op("pe", "matmul", py[:, h * 512:(h + 1) * 512], lhsT=ymix[:, m, t0:t0 + 128], rhs=wo[:, m, h * 512:(h + 1) * 512],
                                     start=(m == 0), stop=(m == 7), reads=[b_ym, b_wo], writes=[b_py[k % 3]], inc=(m == 7))
                        post_norm_add(py, b_py[k % 3], xt[:, s_, :], bx, Gt, b_Gt, tmp, b_tmp, sty, b_sty2[s_], junk, b_junk, s_,
                                      add_engs=("dve", "dve"))
                        k += 1
                    if pre_dmas and i % 2 == 1:
                        pre_dmas.pop(0)()
                    p.dma("pool", x2s[i * 256:(i + 1) * 256, :].rearrange("(s p) d -> p s d", p=128), xt[:, :, :], reads=[bx], key=bx)
                while pre_dmas:
                    pre_dmas.pop(0)()
                p.barrier(keep=([K.ffn2_pre] if K.ffn2_pre is not None else []))
                p.emit()

        K.phase_extra = {"s5setup": phase_s5setup, "m1": phase_m1, "s5": phase_s5, "m2a": phase_m2a, "m2b": phase_m2b,
                         "m2c": phase_m2c, "m2d": phase_m2d}

        run = [ph for ph in all_ph if ph in phases]
        phase_init()
        xtiles = lambda src, dst: [(src[i * 256:(i + 1) * 256, :], dst[i * 256:(i + 1) * 256, :], 2, 0) for i in range(16)]
        for ph in run:
            if ph == "mods":
                phase_mods()
                p.barrier(keep=([K.ffn1_wd] if getattr(K, "ffn1_wd", None) is not None else []))
                p.emit()
            elif ph == "ffn1":
                phase_ffn("a", I.f1_wgu, I.f1_wd, gs1, 0, gc1, [(I.ctx, c1s, 2, 1)] + xtiles(I.x, x1s),
                          pre_kcs=(range(8) if "mods" in phases else ()), pre_wd=("mods" in phases),
                          pre_buf_wd=(getattr(K, "ffn1_wd", None) if "mods" in phases else None))
            elif ph == "ffn2":
                phase_ffn("b", I.f2_wgu, I.f2_wd, gs2, 6, gc2, xtiles(x2s, out),
                          pre_kcs=(range(7) if "m2d" in phases else ()), pre_wd=False,
                          pre_buf=(getattr(K, "ffn2_pre", None) if "m2d" in phases else None))
            else:
                K.phase_extra[ph]()
        with contextlib.ExitStack() as ps:
            fin = Buf("fin")
            if dbg:
                for name, src in (("x1s", x1s), ("c1s", c1s), ("x2s", x2s)):
                    if name in dbg:
                        p.dma("sp", dbg[name], src, writes=[fin], key=fin)
            p.barrier()
            p.emit()
    return nc


K_phase_doc = None


def prep_inputs(inputs, b):
    f = lambda k: np.ascontiguousarray(np.asarray(inputs[k], dtype=np.float32))
    col = lambda v: np.ascontiguousarray(v.reshape(-1, 128).T)
    m = {}
    m["x"] = f("x")[b]
    m["ctx"] = f("ctx")[b]
    cc = np.stack([col(f("c")[b]), col(f("c_ctx"))], axis=-1)
    m["cc"] = np.ascontiguousarray(cc.reshape(128, 16))
    m["w_ada"] = f("w_ada")[0]
    m["bada"] = col(f("b_ada")[0])
    g6 = np.stack([col(f(k)[0]) for k in ("ffn1_g_pre", "ffn1_g_post", "mix_g_pre", "mix_g_post", "ffn2_g_pre", "ffn2_g_post")], axis=-1)
    m["gains"] = np.ascontiguousarray(g6.reshape(128, 48))
    m["ffn1_w_gu"] = f("ffn1_w_gu")[0]
    m["ffn1_w_down"] = f("ffn1_w_down")[0]
    m["ffn2_w_gu"] = f("ffn2_w_gu")[0]
    m["ffn2_w_down"] = f("ffn2_w_down")[0]
    m["w_in"] = f("w_in")[0]
    cw = f("conv_w")[0]
    m["cw"] = np.ascontiguousarray(np.stack([col(cw[k]) for k in range(3)], axis=-1).reshape(128, 12))
    dp = lambda a: np.ascontiguousarray(np.transpose(a, (0, 2, 1)).reshape(128, NG))
    logdt = np.broadcast_to(f("ssm_log_dt")[0][:, :, None], (2, NG, 64))
    m["s5a"] = np.ascontiguousarray(np.stack([dp(f("ssm_lam_re")[0]), dp(f("ssm_lam_im")[0]), dp(logdt)], axis=1).reshape(128, 96))
    bre = np.transpose(f("ssm_b_re")[0], (0, 2, 1, 3)).reshape(128, NG, 16)
    bim = np.transpose(f("ssm_b_im")[0], (0, 2, 1, 3)).reshape(128, NG, 16)
    cre = np.transpose(f("ssm_c_re")[0], (0, 3, 1, 2)).reshape(128, NG, 16)
    cim = np.transpose(f("ssm_c_im")[0], (0, 3, 1, 2)).reshape(128, NG, 16)
    m["s5b"] = np.ascontiguousarray(np.stack([bre, bim, cre, cim], axis=1).reshape(128, 4 * NG * 16))
    dsk = f("ssm_d")[0].reshape(NG, 16)
    m["dcol"] = np.ascontiguousarray(np.tile(dsk.T, (8, 1)))
    m["w_glu"] = f("w_glu")[0]
    m["bglu"] = col(f("b_glu")[0])
    m["w_out"] = f("w_out")[0]
    ii = np.arange(128) // 16
    ident = np.eye(128, dtype=np.float32)
    mask0 = (ii[:, None] <= ii[None, :]).astype(np.float32)
    mask1 = (ii[:, None] >= ii[None, :]).astype(np.float32)
    ex = np.zeros((128, 26), np.float32)
    i8 = np.arange(8)
    ex[:64, 0:8] = 7 - i8
    ex[64:, 0:8] = i8
    ex[:64, 8:16] = i8 + 1
    ex[64:, 8:16] = 8 - i8
    ex[:, 16:24] = ex[:, 8:16] - 8
    ex[:, 24] = 8
    ex[:, 25] = 1
    ramp = np.broadcast_to(np.arange(544, dtype=np.float32), (128, 544))
    m["cst"] = np.ascontiguousarray(np.concatenate([ident, mask0, mask1, ex, ramp], axis=1))
    return m


_NC_CACHE = {}


def kernel(**inputs):
    if "nc" not in _NC_CACHE:
        _NC_CACHE["nc"] = build()
    nc = _NC_CACHE["nc"]
    in_maps = [prep_inputs(inputs, b) for b in range(8)]
    res = run_bass_kernel_spmd(nc, in_maps, core_ids=list(range(8)))
    return np.stack([np.asarray(r["out"], dtype=np.float32) for r in res.results], axis=0)
```

```python
import contextlib
import math
import numpy as np
import concourse.bass as bass
import concourse.mybir as mybir
from concourse.bass_utils import run_bass_kernel_spmd

F32 = mybir.dt.float32
BF16 = mybir.dt.bfloat16
ALU = mybir.AluOpType
AF = mybir.ActivationFunctionType

D = 1024
L = 4096
LC = 256
FF = 2816
KC = 8
FC = 22
NG = 32
EPS = 1e-6
MAGIC = 12582912.0
TWO_PI = 2.0 * math.pi
SB_TOP = 227328
SB_BASE = 16512
import os
NOSELF = os.environ.get('NOSELF', '0') == '1'


class Buf:
    __slots__ = ("name", "W", "R")

    def __init__(self, name):
        self.name = name
        self.W = {}
        self.R = {}


class Prog:
    ENG = ("pe", "act", "dve", "pool", "sp")
    ATTR = {"pe": "tensor", "act": "scalar", "dve": "vector", "pool": "gpsimd", "sp": "sync"}

    def __init__(self, nc, stack):
        self.nc = nc
        self.stack = stack
        self.esem = {e: stack.enter_context(nc.semaphore("s_" + e)) for e in self.ENG}
        self.cnt = {e: 0 for e in self.ENG}
        self.streams = {e: [] for e in self.ENG}
        self.waited = {e: {} for e in self.ENG}
        self.dsems = {}
        self.dcnt = {}
        self.free_dsems = []

    def _dsem(self, key):
        if key not in self.dsems:
            if self.free_dsems:
                s, c = self.free_dsems.pop()
            else:
                s, c = self.stack.enter_context(self.nc.semaphore("d_%d" % len(self.dsems))), 0
            self.dsems[key] = s
            self.dcnt[key] = c
        return self.dsems[key]

    def _need(self, eng, need):
        waits = []
        wd = self.waited[eng]
        for s, v in need.items():
            if wd.get(s, 0) < v:
                wd[s] = v
                waits.append((s, v))
        return waits

    def _deps(self, eng, reads, writes, tok, nowaw=False):
        need = {}

        def add(s, v):
            if need.get(s, 0) < v:
                need[s] = v
        own = self.esem[eng]
        for b in reads:
            for s, v in b.W.items():
                if s is own and (eng in ("pe", "sp") or NOSELF):
                    continue
                add(s, v)
        for b in writes:
            if not nowaw:
                for s, v in b.W.items():
                    if s is not own:
                        add(s, v)
            for s, v in b.R.items():
                if s is not own:
                    add(s, v)
        waits = self._need(eng, need)
        ts, tv = tok
        for b in reads:
            if b.R.get(ts, 0) < tv:
                b.R[ts] = tv
        for b in writes:
            if b.R:
                b.W = {ts: tv}
                b.R = {}
            else:
                b.W[ts] = tv
        return waits

    def op(self, eng, method, *args, reads=(), writes=(), inc=True, nowaw=False, **kw):
        tok = (self.esem[eng], self.cnt[eng] + 1)
        waits = self._deps(eng, reads, writes, tok, nowaw=nowaw)
        if inc:
            self.cnt[eng] += 1
        fn = (lambda e, m=method, a=args, k=kw: getattr(e, m)(*a, **k))
        self.streams[eng].append((waits, fn, self.esem[eng] if inc else None, 1))

    def dma(self, eng, out, in_, reads=(), writes=(), key=None):
        kb = key if key is not None else (writes[0] if writes else reads[0])
        sem = self._dsem(kb)
        self.dcnt[kb] += 16
        tok = (sem, self.dcnt[kb])
        waits = self._deps(eng, reads, writes, tok)
        self.streams[eng].append((waits, lambda e, o=out, i=in_: e.dma_start(out=o, in_=i), sem, 16))

    def barrier(self, keep=()):
        need = {self.esem[e]: self.cnt[e] for e in self.ENG if self.cnt[e] > 0}
        for k, s in self.dsems.items():
            if self.dcnt[k] > 0 and k not in keep:
                need[s] = max(need.get(s, 0), self.dcnt[k])
        for e in self.ENG:
            w = self._need(e, {s: v for s, v in need.items() if s is not self.esem[e]})
            if w:
                self.streams[e].append((w, None, None, 0))
        kept_s = {k: self.dsems[k] for k in self.dsems if k in keep}
        kept_c = {k: self.dcnt[k] for k in self.dsems if k in keep}
        for k in list(self.dsems):
            if k not in keep:
                self.free_dsems.append((self.dsems[k], self.dcnt[k]))
        self.dsems = kept_s
        self.dcnt = kept_c

    def emit(self):
        nc = self.nc
        with nc.Block() as block:
            for e in self.ENG:
                stream = self.streams[e]

                def body(eo, stream=stream):
                    for waits, fn, sem, incv in stream:
                        for s, v in waits:
                            eo.wait_ge(s, v)
                        if fn is None:
                            continue
                        ins = fn(eo)
                        if sem is not None:
                            ins.then_inc(sem, incv)
                getattr(block, self.ATTR[e])(body)
        self.streams = {e: [] for e in self.ENG}


class Ctx:
    pass


class _Stop(Exception):
    pass


import os


def _cut(n):
    if int(os.environ.get('S5STOP', '99')) == n:
        raise _Stop()


def _mk(ap, dims):
    return bass.AP(ap.tensor, ap.offset, [list(ap.ap[0])] + [list(d) for d in dims])


def build(debug=(), inject=(), phases=None):
    nc = bass.Bass("TRN2", target_bir_lowering=False)
    K = Ctx()
    K.nc = nc
    K.uid = 0
    all_ph = ["mods", "ffn1", "s5setup", "m1", "s5", "m2a", "m2b", "m2c", "m2d", "ffn2"]
    phases = all_ph if phases is None else phases

    def din(name, shape):
        return nc.dram_tensor(name, shape, F32, kind="ExternalInput").ap()

    def dscratch(name, shape):
        kind = "ExternalInput" if name in inject else "Internal"
        return nc.dram_tensor(name, shape, F32, kind=kind).ap()

    I = Ctx()
    I.x = din("x", [L, D])
    I.ctx = din("ctx", [LC, D])
    I.cc = din("cc", [128, 16])
    I.w_ada = din("w_ada", [D, 9 * D])
    I.bada = din("bada", [128, 72])
    I.gains = din("gains", [128, 48])
    I.f1_wgu = din("ffn1_w_gu", [D, 2 * FF])
    I.f1_wd = din("ffn1_w_down", [FF, D])
    I.f2_wgu = din("ffn2_w_gu", [D, 2 * FF])
    I.f2_wd = din("ffn2_w_down", [FF, D])
    I.w_in = din("w_in", [D, 2048])
    I.cw = din("cw", [128, 12])
    I.s5a = din("s5a", [128, 96])
    I.s5b = din("s5b", [128, 4 * NG * 16])
    I.dcol = din("dcol", [128, NG])
    I.w_glu = din("w_glu", [512, 512])
    I.bglu = din("bglu", [128, 4])
    I.w_out = din("w_out", [D, D])
    I.cst = din("cst", [128, 128 * 3 + 26 + 544])
    out = nc.dram_tensor("out", [L, D], F32, kind="ExternalOutput").ap()
    x1s = dscratch("x1s", [L, D])
    c1s = dscratch("c1s", [LC, D])
    x2s = dscratch("x2s", [L, D])
    dbg = {}
    for name, shape in debug:
        dbg[name] = nc.dram_tensor("dbg_" + name, shape, F32, kind="ExternalOutput").ap()

    with contextlib.ExitStack() as st:
        p = Prog(nc, st)
        K.p = p

        def sb(name, shape, dtype, off):
            K.uid += 1
            nbytes = int(np.prod(shape[1:])) * mybir.dt.size(dtype)
            assert off % 32 == 0 and off + nbytes <= SB_TOP, (name, off, nbytes)
            return nc.alloc_sbuf_tensor_at("%s_%d" % (name, K.uid), shape, dtype, offset=off)

        class Arena:
            def __init__(self, base, top):
                self.ptr = base
                self.top = top

            def al(self, name, shape, dtype):
                nbytes = int(np.prod(shape[1:])) * mybir.dt.size(dtype)
                nbytes = (nbytes + 31) // 32 * 32
                t = sb(name, shape, dtype, self.ptr)
                self.ptr += nbytes
                assert self.ptr <= self.top, (name, self.ptr, self.top)
                return t

        PA = Arena(SB_BASE, SB_BASE + 6144)
        ident = PA.al("ident", [128, 128], F32)
        identb = PA.al("identb", [128, 128], BF16)
        onesf = PA.al("onesf", [128, 128], F32)
        epsT = PA.al("epsT", [128, 1], F32)
        mhalf = PA.al("mhalf", [128, 16], F32)
        magt = PA.al("magt", [128, 4], F32)
        modsT = PA.al("modsT", [128, 72, 2], F32)
        gains = PA.al("gains", [128, 8, 6], F32)
        gs1 = PA.al("gs1", [128, 8, 2], F32)
        gsm = PA.al("gsm", [128, 8, 2], F32)
        gs2 = PA.al("gs2", [128, 8, 2], F32)
        gc1 = PA.al("gc1", [128, 8, 2], F32)
        gcm = PA.al("gcm", [128, 8, 2], F32)
        gc2 = PA.al("gc2", [128, 8, 2], F32)
        cwt = PA.al("cwt", [128, 4, 3], F32)
        bglut = PA.al("bglut", [128, 4], F32)
        B_const = Buf("const")
        B_mods = Buf("mods")
        ARENA0 = SB_BASE + 6144

        OFF_YMIX = SB_TOP - 65536
        OFF_YTM = OFF_YMIX - 32768
        OFF_MATS = OFF_YTM - 45056
        OFF_BGS = OFF_YTM - 32768

        def dbg_dump(name, src_ap, reads, eng="sp"):
            if name in dbg:
                p.dma(eng, dbg[name], src_ap, reads=reads, key=Buf("dbgk"))

        def shcol(j, kc, t):
            return modsT[:, j * 8 + kc, t:t + 1]

        def phase_init():
            p.dma("sp", ident[:], I.cst[:, 0:128], writes=[B_const])
            p.dma("sp", gains[:].rearrange("p a b -> p (a b)"), I.gains, writes=[B_const])
            p.dma("sp", cwt[:].rearrange("p a b -> p (a b)"), I.cw, writes=[B_const])
            p.dma("sp", bglut[:], I.bglu, writes=[B_const])
            p.op("dve", "tensor_copy", out=identb[:], in_=ident[:], reads=[B_const], writes=[B_const])
            p.op("pool", "memset", onesf[:], 1.0, writes=[B_const])
            p.op("pool", "memset", epsT[:], EPS, writes=[B_const])
            p.op("pool", "memset", mhalf[:], -0.5, writes=[B_const])
            p.op("pool", "memset", magt[:, 0:1], MAGIC, writes=[B_const])
            p.op("pool", "memset", magt[:, 1:2], -MAGIC, writes=[B_const])
            p.op("pool", "memset", magt[:, 2:3], 0.25, writes=[B_const])
            p.barrier()
            p.emit()

        def phase_mods():
            pre = "ffn1" in phases
            K.ffn1_wd = None
            if pre:
                _, K.ffn1_wd = ffn_weight_dmas(I.f1_wgu, I.f1_wd, range(8), True, split=True)
            A = Arena(ARENA0 + (135168 if pre else 0), SB_TOP)
            cct = A.al("cct", [128, 16], F32)
            scc = A.al("scc", [128, 2, 8], F32)
            badat = A.al("badat", [128, 72], F32)
            rr = [A.al("rr", [2, 512], F32)] * 2
            wst = [A.al("wst%d" % i, [128, 8, 1024], F32) for i in range(2)]
            with contextlib.ExitStack() as ps:
                pm = ps.enter_context(nc.psum_tensor("pm", [128, 144], F32))
                prow = [ps.enter_context(nc.psum_tensor("prw%d" % i, [2, 512], F32)) for i in range(2)]
                b_cc, b_scc, b_bada, b_pm, b_g = Buf("cc"), Buf("scc"), Buf("bada"), Buf("pm"), B_const
                b_w = [Buf("wst0"), Buf("wst1")]
                b_prow = [Buf("prow0"), Buf("prow1")]
                b_rr = [Buf("rr0")] * 2
                p.dma("sp", cct[:], I.cc, writes=[b_cc])
                p.dma("sp", badat[:], I.bada, writes=[b_bada])
                p.op("act", "activation", out=scc[:].rearrange("p t k -> p k t"), in_=cct[:].rearrange("p (k t) -> p k t", t=2), func=AF.Silu,
                     reads=[b_cc], writes=[b_scc])
                wv = I.w_ada.rearrange("(kc p) n -> p kc n", p=128)
                k = 0
                for j in range(9):
                    w = wst[j % 2]
                    for kc in range(8):
                        p.dma("sp", w[:, kc, :], wv[:, kc, j * 1024:(j + 1) * 1024], writes=[b_w[j % 2]])
                    for hb in range(2):
                        pr = prow[k % 2]
                        for kc in range(8):
                            p.op("pe", "matmul", pr[:, :], lhsT=scc[:, :, kc], rhs=w[:, kc, hb * 512:(hb + 1) * 512],
                                 start=(kc == 0), stop=(kc == 7), reads=[b_w[j % 2], b_scc], writes=[b_prow[k % 2]], inc=(kc == 7))
                        p.op("act", "copy", out=rr[k % 2][:, :], in_=pr[:, :], reads=[b_prow[k % 2]], writes=[b_rr[k % 2]])
                        for nch in range(4):
                            col = (j * 8 + hb * 4 + nch) * 2
                            p.op("pe", "matmul", pm[:, col:col + 2], lhsT=rr[k % 2][:, nch * 128:(nch + 1) * 128], rhs=ident[0:2, 0:2],
                                 start=True, stop=True, reads=[b_rr[k % 2], B_const], writes=[b_pm])
                        k += 1
                p.op("dve", "tensor_tensor", out=modsT[:], in0=pm[:].rearrange("p (n t) -> p n t", t=2),
                     in1=_mk(badat[:, 0:1], [[1, 72], [0, 2]]), op=ALU.add, reads=[b_pm, b_bada], writes=[B_mods])
                mv = modsT[:].rearrange("p (j k) t -> p j k t", j=9)

                def gcolb(idx):
                    return _mk(gains[:, 0, idx:idx + 1], [[6, 8], [0, 2]])
                for dst, jscale, gi in ((gs1, 1, 0), (gsm, 4, 2), (gs2, 7, 4)):
                    p.op("dve", "scalar_tensor_tensor", out=dst[:], in0=mv[:, jscale], scalar=1.0, in1=gcolb(gi),
                         op0=ALU.add, op1=ALU.mult, reads=[B_mods, b_g], writes=[B_mods])
                for dst, jg, gi, sc in ((gc1, 2, 1, 0.5), (gcm, 5, 3, 1.0), (gc2, 8, 5, 0.5)):
                    p.op("dve", "scalar_tensor_tensor", out=dst[:], in0=mv[:, jg], scalar=sc, in1=gcolb(gi),
                         op0=ALU.mult, op1=ALU.mult, reads=[B_mods, b_g], writes=[B_mods])
                if "modsT" in dbg:
                    p.dma("sp", dbg["modsT"], modsT[:].rearrange("p n t -> p (n t)"), reads=[B_mods], key=Buf("k"))
                p.emit()

        def make_row(dst, col_of_kc, bdst, prow, bpr, dg, bdg):
            for kc in range(8):
                p.op("dve", "tensor_scalar", out=dg[kc % 2][:], in0=ident[:], scalar1=col_of_kc(kc), scalar2=None,
                     op0=ALU.mult, reads=[B_const, B_mods], writes=[bdg[kc % 2]])
                p.op("pe", "matmul", prow[:, kc * 128:(kc + 1) * 128], lhsT=onesf[:], rhs=dg[kc % 2][:], start=True, stop=True,
                     reads=[bdg[kc % 2], B_const], writes=[bpr])
            p.op("act", "copy", out=dst[:], in_=prow[:], reads=[bpr], writes=[bdst])

        def norm_T(xt, npart, nsub, gs, sh_j, t, bufs):
            norm_A(xt, npart, nsub, bufs)
            norm_B(npart, nsub, gs, sh_j, t, bufs)

        def norm_A(xt, npart, nsub, bufs):
            for st_ in norm_A_steps(xt, npart, nsub, bufs):
                st_()

        def norm_A_steps(xt, npart, nsub, bufs):
            (junk, ssq, rstd, xnb, hT, ptr, b_xt, b_st, b_xnb, b_hT, b_ptr) = bufs
            steps = []
            for s in range(nsub):
                steps.append(lambda s=s: p.op("act", "activation", out=junk[:npart, :], in_=xt[:npart, s, :], func=AF.Square,
                                              accum_out=ssq[:npart, s:s + 1], reads=[b_xt], writes=[b_st]))

            def rs():
                if getattr(K, "rs_mode", "pow") == "sqrt":
                    p.op("act", "activation", out=rstd[:npart, :nsub], in_=ssq[:npart, :nsub], func=AF.Sqrt, scale=1.0 / D,
                         bias=epsT[:npart, 0:1], reads=[b_st, B_const], writes=[b_st])
                    p.op("dve", "reciprocal", out=rstd[:npart, :nsub], in_=rstd[:npart, :nsub], reads=[b_st], writes=[b_st])
                    return
                p.op("dve", "tensor_scalar", out=rstd[:npart, :nsub], in0=ssq[:npart, :nsub], scalar1=1.0 / D, scalar2=EPS,
                     op0=ALU.mult, op1=ALU.add, reads=[b_st], writes=[b_st])
                p.op("pool", "tensor_tensor", out=rstd[:npart, :nsub], in0=rstd[:npart, :nsub], in1=mhalf[:npart, :nsub], op=ALU.pow,
                     reads=[b_st, B_const], writes=[b_st])
            steps.append(rs)
            for s in range(nsub):
                if s % 2 == 0:
                    steps.append(lambda s=s: p.op("dve", "tensor_scalar", out=xnb[:npart, s, :], in0=xt[:npart, s, :],
                                                  scalar1=rstd[:npart, s:s + 1], scalar2=None, op0=ALU.mult, reads=[b_xt, b_st], writes=[b_xnb], nowaw=True))
                else:
                    steps.append(lambda s=s: p.op("act", "activation", out=xnb[:npart, s, :], in_=xt[:npart, s, :], func=AF.Copy,
                                                  scale=rstd[:npart, s:s + 1], reads=[b_xt, b_st], writes=[b_xnb], nowaw=True))
            return steps

        def norm_B(npart, nsub, gs, sh_j, t, bufs):
            for st_ in norm_B_steps(npart, nsub, gs, sh_j, t, bufs):
                st_()

        def norm_B_steps(npart, nsub, gs, sh_j, t, bufs):
            return [(lambda kc=kc: norm_B_kc(kc, npart, nsub, gs, sh_j, t, bufs)) for kc in range(8)]

        def norm_B_kc(kc, npart, nsub, gs, sh_j, t, bufs):
            (junk, ssq, rstd, xnb, hT, ptr, b_xt, b_st, b_xnb, b_hT, b_ptr) = bufs
            if True:
                pt = ptr[kc % 2]
                for s in range(nsub):
                    p.op("pe", "transpose", out=pt[:, s, :npart], in_=xnb[:npart, s, kc * 128:(kc + 1) * 128],
                         identity=identb[:npart, :npart], reads=[b_xnb, B_const], writes=[b_ptr[kc % 2]], inc=(s == nsub - 1))
                if nsub == 8 and kc % 2 == 1:
                    p.op("dve", "tensor_scalar", out=hT[:, kc, :nsub, :npart], in0=pt[:, :nsub, :npart], scalar1=gs[:, kc, t:t + 1],
                         scalar2=shcol(sh_j, kc, t), op0=ALU.mult, op1=ALU.add, reads=[b_ptr[kc % 2], B_mods], writes=[b_hT], nowaw=True)
                else:
                    p.op("act", "activation", out=hT[:, kc, :nsub, :npart], in_=pt[:, :nsub, :npart], func=AF.Identity,
                         scale=gs[:, kc, t:t + 1], bias=shcol(sh_j, kc, t), reads=[b_ptr[kc % 2], B_mods], writes=[b_hT], nowaw=True)

        def post_norm_add(py, b_py, xrow, b_x, G, b_G, tmp, b_tmp, sty, b_sty, junk, b_junk, s, add_engs=("pool", "pool")):
            p.op("act", "activation", out=junk[:, :], in_=py[:, :], func=AF.Square, accum_out=sty[:, s:s + 1],
                 reads=[b_py], writes=[b_sty, b_junk])
            if getattr(K, "rs_mode", "pow") == "sqrt":
                p.op("act", "activation", out=sty[:, 8 + s:9 + s], in_=sty[:, s:s + 1], func=AF.Sqrt, scale=1.0 / D,
                     bias=epsT[:, 0:1], reads=[b_sty, B_const], writes=[b_sty])
                p.op("dve", "reciprocal", out=sty[:, 8 + s:9 + s], in_=sty[:, 8 + s:9 + s], reads=[b_sty], writes=[b_sty])
            else:
                p.op("dve", "tensor_scalar", out=sty[:, 8 + s:9 + s], in0=sty[:, s:s + 1], scalar1=1.0 / D, scalar2=EPS,
                     op0=ALU.mult, op1=ALU.add, reads=[b_sty], writes=[b_sty])
                p.op("pool", "tensor_tensor", out=sty[:, 8 + s:9 + s], in0=sty[:, 8 + s:9 + s], in1=mhalf[:, 0:1], op=ALU.pow,
                     reads=[b_sty, B_const], writes=[b_sty])
            for h in range(2):
                sl = slice(h * 512, (h + 1) * 512)
                p.op("dve", "scalar_tensor_tensor", out=tmp[h][:, :], in0=py[:, sl], scalar=sty[:, 8 + s:9 + s], in1=G[:, sl],
                     op0=ALU.mult, op1=ALU.mult, reads=[b_py, b_sty, b_G], writes=[b_tmp[h]])
            for h in range(2):
                sl = slice(h * 512, (h + 1) * 512)
                p.op(add_engs[h], "tensor_tensor", out=xrow[:, sl], in0=xrow[:, sl], in1=tmp[h][:, :], op=ALU.add,
                     reads=[b_tmp[h], b_x], writes=[b_x])

        def ffn_weight_dmas(wgu_d, wd_d, kcs, do_wd, deferred=False, split=False):
            A = Arena(ARENA0, SB_TOP)
            wg = A.al("wgp", [128, 8, 2 * FF], BF16)
            wd = A.al("wdp", [128, FC, D], BF16)
            bb = Buf("ffnw_pre")
            bb_wd = Buf("ffnw_pre_wd") if split else bb
            todo = []
            for kc in kcs:
                todo.append(lambda kc=kc: p.dma("pool", wg[:, kc, :], wgu_d[kc * 128:(kc + 1) * 128, :], writes=[bb]))
            if do_wd:
                wdv = wd_d.rearrange("(fc p) n -> p fc n", p=128)
                for h in range(2):
                    todo.append(lambda h=h: p.dma("pool", wd[:, h * 11:(h + 1) * 11, :], wdv[:, h * 11:(h + 1) * 11, :], writes=[bb_wd]))
            if deferred:
                return bb, todo
            for f in todo:
                f()
            return (bb, bb_wd) if split else bb

        def phase_ffn(tag, wgu_d, wd_d, gs, sh_j, gc, tiles, pre_kcs=(), pre_wd=False, pre_buf=None, pre_buf_wd=None):
            K.rs_mode = "pow"
            A = Arena(ARENA0, SB_TOP)
            wg = A.al("wg", [128, 8, 2 * FF], BF16)
            wd = A.al("wd", [128, FC, D], BF16)
            xts = [A.al("xt", [128, 2, D], F32) for _ in range(3)]
            xnb = A.al("xnb", [128, 2, D], BF16)
            hTs = [A.al("hT", [128, 8, 2, 128], BF16) for _ in range(2)]
            aT = A.al("aT", [128, FC, 256], BF16)
            sg = [A.al("sg", [128, 512], F32) for _ in range(2)]
            tmp = [A.al("tmp", [128, 512], F32) for _ in range(2)]
            dg = [A.al("dg", [128, 128], F32) for _ in range(2)]
            junk = A.al("junk", [128, D], BF16)
            ssq = A.al("ssq", [128, 8], F32)
            rstd = A.al("rstd", [128, 8], F32)
            sty = A.al("sty", [128, 16], F32)
            Gt = A.al("G", [128, D], F32)
            with contextlib.ExitStack() as ps:
                b_Gt = Buf("G")
                bdg = [Buf("dg0"), Buf("dg1")]
                ptr = [ps.enter_context(nc.psum_tensor("ptr%s%d" % (tag, i), [128, 8, 128], BF16)) for i in range(2)]
                if os.environ.get("PGU", "0") == "1":
                    pgu = [ps.enter_context(nc.psum_tensor("pgu%s%d" % (tag, i), [128, 2, 256], F32)) for i in range(2)]
                    pys = [ps.enter_context(nc.psum_tensor("py%s%d" % (tag, i), [128, D], F32)) for i in range(2)]
                elif os.environ.get("PGU", "0") == "2":
                    pgu = [ps.enter_context(nc.psum_tensor("pgu%s%d" % (tag, i), [128, 2, 256], F32)) for i in range(2)]
                    pys = [ps.enter_context(nc.psum_tensor("py%s%d" % (tag, 0), [128, D], F32))] * 2
                else:
                    pgu = [ps.enter_context(nc.psum_tensor("pgu%s%d" % (tag, i), [128, 2, 512], F32)) for i in range(2)]
                    pys = [ps.enter_context(nc.psum_tensor("py%s%d" % (tag, 0), [128, D], F32))] * 2
                b_wg = [(pre_buf if (pre_buf is not None and k in pre_kcs) else Buf("wg%d" % k)) for k in range(8)]
                b_wd = [(pre_buf if (pre_buf is not None and pre_wd) else Buf("wd%d" % k)) for k in range(2)]
                if pre_buf_wd is not None:
                    b_wd = [pre_buf_wd, pre_buf_wd]
                b_xt = [Buf("xt0"), Buf("xt1"), Buf("xt2")]
                b_st, b_xnb, b_aT, b_sty, b_junk = (Buf(n) for n in ("st", "xnb", "aT", "sty", "junk"))
                b_hTs = [Buf("hT0"), Buf("hT1")]
                b_pys = [Buf("py0"), Buf("py1")]
                if os.environ.get("PGU", "0") != "1" or os.environ.get("PYSER", "0") == "1":
                    b_pys = [b_pys[0], b_pys[0]]
                b_pgu = [Buf("pgu0"), Buf("pgu1")]
                b_ptr = [Buf("ptr0"), Buf("ptr1")]
                b_sg = [Buf("sg0"), Buf("sg1")]
                b_tmp = [Buf("tmp0"), Buf("tmp1")]
                cur_t = [None]
                for kc in range(8):
                    if kc not in pre_kcs:
                        p.dma("pool", wg[:, kc, :], wgu_d[kc * 128:(kc + 1) * 128, :], writes=[b_wg[kc]])
                wdv = wd_d.rearrange("(fc p) n -> p fc n", p=128)
                for h in range(2):
                    if not pre_wd:
                        p.dma("pool", wd[:, h * 11:(h + 1) * 11, :], wdv[:, h * 11:(h + 1) * 11, :], writes=[b_wd[h]])

                def load(i):
                    src, _, nsub, _ = tiles[i]
                    p.dma("sp", xts[i % 3][:, :nsub, :], src.rearrange("(s p) d -> p s d", p=128), writes=[b_xt[i % 3]])
                def nbufs(i):
                    return (junk, ssq, rstd, xnb, hTs[i % 2], ptr, b_xt[i % 3], b_st, b_xnb, b_hTs[i % 2], b_ptr)

                def pe_up(i, fc):
                    N = tiles[i][2] * 128
                    hTf = hTs[i % 2][:].rearrange("p k s c -> p k (s c)")
                    pp = pgu[fc % 2]
                    for which in range(2):
                        c0 = which * FF + fc * 128
                        for kc in range(8):
                            p.op("pe", "matmul", pp[:, which, :N], lhsT=wg[:, kc, c0:c0 + 128], rhs=hTf[:, kc, :N],
                                 start=(kc == 0), stop=(kc == 7), reads=[b_wg[kc], b_hTs[i % 2]], writes=[b_pgu[fc % 2]],
                                 inc=(which == 1 and kc == 7))

                def evac_up(i, fc):
                    N = tiles[i][2] * 128
                    pp = pgu[fc % 2]
                    p.op("act", "activation", out=sg[fc % 2][:, :N], in_=pp[:, 0, :N], func=AF.Silu,
                         reads=[b_pgu[fc % 2]], writes=[b_sg[fc % 2]])
                    p.op("dve", "tensor_tensor", out=aT[:, fc, :N], in0=sg[fc % 2][:, :N], in1=pp[:, 1, :N], op=ALU.mult,
                         reads=[b_sg[fc % 2], b_pgu[fc % 2]], writes=[b_aT])

                def down(i, s):
                    py = pys[0]
                    for h in range(2):
                        for fc in range(FC):
                            p.op("pe", "matmul", py[:, h * 512:(h + 1) * 512], lhsT=aT[:, fc, s * 128:(s + 1) * 128],
                                 rhs=wd[:, fc, h * 512:(h + 1) * 512], start=(fc == 0), stop=(fc == FC - 1),
                                 reads=[b_aT, b_wd[fc // 11]], writes=[b_pys[0]], inc=(fc == FC - 1))
                load(0)
                norm_T(xts[0], 128, tiles[0][2], gs, sh_j, tiles[0][3], nbufs(0))
                for fc in range(2):
                    pe_up(0, fc)
                    evac_up(0, fc)
                nT = len(tiles)
                for i, (src, dst, nsub, t) in enumerate(tiles):
                    xt = xts[i % 3]
                    bx = b_xt[i % 3]
                    stepsA, stepsB = [], []
                    if i + 1 < nT:
                        load(i + 1)
                        stepsA = norm_A_steps(xts[(i + 1) % 3], 128, tiles[i + 1][2], nbufs(i + 1))
                        stepsB = norm_B_steps(128, tiles[i + 1][2], gs, sh_j, tiles[i + 1][3], nbufs(i + 1))
                    if cur_t[0] != t:
                        make_row(Gt, lambda kc, t=t: gc[:, kc, t:t + 1], b_Gt, pys[0], b_pys[0], dg, bdg)
                        cur_t[0] = t
                    for fc in range(2, FC):
                        pe_up(i, fc)
                        evac_up(i, fc)
                        if fc % 2 == 0 and fc <= 10 and stepsA:
                            stepsA.pop(0)()
                        if fc >= 13:
                            while stepsA:
                                stepsA.pop(0)()
                            if stepsB:
                                stepsB.pop(0)()
                    while stepsA:
                        stepsA.pop(0)()
                    while stepsB:
                        stepsB.pop(0)()
                    down(i, 0)
                    post_norm_add(pys[0], b_pys[0], xt[:, 0, :], bx, Gt, b_Gt, tmp, b_tmp, sty, b_sty, junk, b_junk, 0)
                    if i + 1 < nT:
                        pe_up(i + 1, 0)
                        pe_up(i + 1, 1)
                    down(i, 1)
                    if i + 1 < nT:
                        evac_up(i + 1, 0)
                        evac_up(i + 1, 1)
                    post_norm_add(pys[0], b_pys[0], xt[:, 1, :], bx, Gt, b_Gt, tmp, b_tmp, sty, b_sty, junk, b_junk, 1)
                    p.dma("pool", dst.rearrange("(s p) d -> p s d", p=128), xt[:, :nsub, :], reads=[bx], key=bx)
                p.barrier()
                p.emit()

        MA = Arena(OFF_MATS, OFF_YTM)
        MBre = MA.al("MBre", [128, NG, 128], BF16)
        MBim = MA.al("MBim", [128, NG, 128], BF16)
        QAre = MA.al("QAre", [128, NG, 128], BF16)
        QAim = MA.al("QAim", [128, NG, 128], BF16)
        Tz = MA.al("Tz", [128, NG, 128], BF16)
        rcol = MA.al("rcol", [128, NG], F32)
        psi = MA.al("psi", [128, NG], F32)
        cidx = MA.al("cidx", [128, 544], F32)
        B_mats = Buf("mats")
        Xx = sb("Xx", [128, NG, 512], BF16, OFF_YMIX)
        Xc = sb("Xc", [128, NG, 32], BF16, OFF_YMIX + 32768)
        XFREE = OFF_YMIX + 32768 + 2048
        Ytm = sb("Ytm", [128, 4, 8, 512], BF16, OFF_YTM)
        ymix = sb("ymix", [128, 8, L], BF16, OFF_YMIX)
        bgs = sb("bgs", [128, 4, L], BF16, OFF_BGS)

        def phase_s5setup():
            A = Arena(ARENA0, OFF_MATS)
            Bg = Arena(OFF_YTM, SB_TOP)
            s5a = A.al("s5a", [128, 3, NG], F32)
            s5b = A.al("s5b", [128, 4, NG, 16], F32)
            expo = A.al("expo", [128, 26], F32)
            mask = A.al("mask", [128, 2, 128], F32)
            dcol = A.al("dcol", [128, NG], F32)
            sm = {n: A.al(n, [128, NG], F32) for n in ("dt", "ar", "th", "k", "nr", "den", "t1", "t2", "qre", "qim")}
            big = {n: A.al(n, [128, NG, 26], F32) for n in ("AR", "MAG", "T", "T2", "Kk", "SIN", "COS", "PWre", "PWim")}
            Bb = {n: A.al(n, [128, NG, 16], F32) for n in ("Bbre", "Bbim", "b1", "b2")}
            P_re = Bg.al("P_re", [128, NG, 128], F32)
            P_im = Bg.al("P_im", [128, NG, 128], F32)
            Q8re = Bg.al("Q8re", [128, NG, 128], F32)
            nQ8im = Bg.al("nQ8im", [128, NG, 128], F32)
            tA = Bg.al("tA", [128, NG, 128], F32)
            tB = Bg.al("tB", [128, NG, 128], F32)
            rm = A.al("rm", [128, 2], F32)
            tzt = [A.al("tzt%d" % i, [128, 128], F32) for i in range(2)]
            tzu = [A.al("tzu%d" % i, [128, 128], F32) for i in range(2)]
            b_tzu = [Buf("tzu0"), Buf("tzu1")]
            b = Buf("s5s")
            bP, bQ, btA, btB = Buf("P"), Buf("Q"), Buf("tA"), Buf("tB")
            b_in = Buf("s5in")
            with contextlib.ExitStack() as ps:
                pT = [ps.enter_context(nc.psum_tensor("pT%d" % i, [128, 4, 128], F32)) for i in range(2)]
                ptz = [[ps.enter_context(nc.psum_tensor("ptz%d%d" % (i, j), [128, 2, 128], F32)) for j in range(2)] for i in range(2)]
                b_pT = [Buf("pT0"), Buf("pT1")]
                b_ptz = [Buf("ptz0"), Buf("ptz1")]
                b_tzt = [Buf("tzt0"), Buf("tzt1")]
                try:
                    cst = I.cst
                    p.dma("sp", s5a[:].rearrange("p a g -> p (a g)"), I.s5a, writes=[b_in])
                    p.dma("sp", s5b[:].rearrange("p a g h -> p (a g h)"), I.s5b, writes=[b_in])
                    p.dma("sp", mask[:].rearrange("p a n -> p (a n)"), cst[:, 128:384], writes=[b_in])
                    p.dma("sp", expo[:], cst[:, 384:410], writes=[b_in])
                    p.dma("sp", cidx[:], cst[:, 410:954], writes=[B_mats])
                    p.dma("sp", dcol[:], I.dcol, writes=[b_in])
                    lre, lim, ldt = s5a[:, 0, :], s5a[:, 1, :], s5a[:, 2, :]

                    def tt(out, a, bb, op, eng="dve"):
                        p.op(eng, "tensor_tensor", out=out, in0=a, in1=bb, op=op, reads=[b, b_in], writes=[b])

                    def ts(out, a, s1, op0, s2=None, op1=None):
                        kw = dict(out=out, in0=a, scalar1=s1, scalar2=s2, op0=op0)
                        if op1 is not None:
                            kw["op1"] = op1
                        p.op("dve", "tensor_scalar", reads=[b, b_in], writes=[b], **kw)

                    def rnd_frac(dst, src, k):
                        ts(k, src, MAGIC, ALU.add)
                        ts(k, k, -MAGIC, ALU.add)
                        tt(dst, src, k, ALU.subtract)
                    p.op("act", "activation", out=sm["dt"][:], in_=ldt, func=AF.Exp, reads=[b_in], writes=[b])
                    tt(sm["ar"][:], lre, sm["dt"][:], ALU.mult)
                    p.op("dve", "scalar_tensor_tensor", out=sm["th"][:], in0=lim, scalar=1.0 / TWO_PI, in1=sm["dt"][:],
                         op0=ALU.mult, op1=ALU.mult, reads=[b, b_in], writes=[b])
                    rnd_frac(sm["th"][:], sm["th"][:], sm["k"][:])
                    _cut(1)
                    e_b = _mk(expo[:, 0:1], [[0, NG], [1, 26]])

                    def gb(t):
                        return _mk(t[:, 0:1], [[1, NG], [0, 26]])
                    tt(big["AR"][:], gb(sm["ar"]), e_b, ALU.mult)
                    p.op("act", "activation", out=big["MAG"][:], in_=big["AR"][:], func=AF.Exp, reads=[b], writes=[b])
                    tt(big["T"][:], gb(sm["th"]), e_b, ALU.mult)
                    rnd_frac(big["SIN"][:], big["T"][:], big["Kk"][:])
                    ts(big["T2"][:], big["T"][:], 0.25, ALU.add)
                    rnd_frac(big["COS"][:], big["T2"][:], big["Kk"][:])
                    p.op("dve", "tensor_copy", out=psi[:], in_=big["SIN"][:, :, 24], reads=[b], writes=[B_mats])
                    p.op("dve", "tensor_copy", out=rcol[:], in_=big["MAG"][:, :, 24], reads=[b], writes=[B_mats])
                    p.op("act", "activation", out=big["SIN"][:], in_=big["SIN"][:], func=AF.Sin, scale=TWO_PI, reads=[b], writes=[b])
                    p.op("act", "activation", out=big["COS"][:], in_=big["COS"][:], func=AF.Sin, scale=TWO_PI, reads=[b], writes=[b])
                    tt(big["PWre"][:], big["MAG"][:], big["COS"][:], ALU.mult)
                    tt(big["PWim"][:], big["MAG"][:], big["SIN"][:], ALU.mult)
                    _cut(2)
                    a_re, a_im = big["PWre"][:, :, 25], big["PWim"][:, :, 25]
                    ts(sm["nr"][:], a_re, -1.0, ALU.add)
                    tt(sm["t1"][:], lre, lre, ALU.mult)
                    tt(sm["t2"][:], lim, lim, ALU.mult)
                    tt(sm["den"][:], sm["t1"][:], sm["t2"][:], ALU.add)
                    p.op("dve", "reciprocal", out=sm["den"][:], in_=sm["den"][:], reads=[b], writes=[b])
                    tt(sm["t1"][:], sm["nr"][:], lre, ALU.mult)
                    tt(sm["t2"][:], a_im, lim, ALU.mult)
                    tt(sm["qre"][:], sm["t1"][:], sm["t2"][:], ALU.add)
                    tt(sm["qre"][:], sm["qre"][:], sm["den"][:], ALU.mult)
                    tt(sm["t1"][:], a_im, lre, ALU.mult)
                    tt(sm["t2"][:], sm["nr"][:], lim, ALU.mult)
                    tt(sm["qim"][:], sm["t1"][:], sm["t2"][:], ALU.subtract)
                    tt(sm["qim"][:], sm["qim"][:], sm["den"][:], ALU.mult)

                    def qb(t):
                        return _mk(t[:, 0:1], [[1, NG], [0, 16]])
                    Bre, Bim, Cre, Cim = (s5b[:, i] for i in range(4))
                    tt(Bb["b1"][:], qb(sm["qre"]), Bre, ALU.mult)
                    tt(Bb["b2"][:], qb(sm["qim"]), Bim, ALU.mult)
                    tt(Bb["Bbre"][:], Bb["b1"][:], Bb["b2"][:], ALU.subtract)
                    tt(Bb["b1"][:], qb(sm["qre"]), Bim, ALU.mult)
                    tt(Bb["b2"][:], qb(sm["qim"]), Bre, ALU.mult)
                    tt(Bb["Bbim"][:], Bb["b1"][:], Bb["b2"][:], ALU.add)

                    def pw(t, c0):
                        return _mk(t[:, 0, c0:c0 + 1], [[26, NG], [1, 8], [0, 16]])

                    def hb(ap3):
                        return _mk(ap3[:, 0, 0:1], [[16, NG], [0, 8], [1, 16]])

                    def v4(t):
                        return t[:].rearrange("p g (i h) -> p g i h", h=16)

                    def cprod(dst_re, dst_im, pre, pim, xre, xim, neg_im=False, engs=("dve", "dve")):
                        e0, e1 = engs
                        p.op(e0, "tensor_tensor", out=v4(tA), in0=pre, in1=xre, op=ALU.mult, reads=[b, b_in], writes=[btA])
                        p.op(e1, "tensor_tensor", out=v4(tB), in0=pim, in1=xim, op=ALU.mult, reads=[b, b_in], writes=[btB])
                        p.op(e0, "tensor_tensor", out=dst_re[0], in0=v4(tA), in1=v4(tB), op=ALU.subtract, reads=[btA, btB], writes=[dst_re[1]])
                        p.op(e0, "tensor_tensor", out=v4(tA), in0=pre, in1=xim, op=ALU.mult, reads=[b, b_in], writes=[btA])
                        p.op(e1, "tensor_tensor", out=v4(tB), in0=pim, in1=xre, op=ALU.mult, reads=[b, b_in], writes=[btB])
                        if neg_im:
                            p.op("dve", "scalar_tensor_tensor", out=dst_im[0], in0=v4(tA), scalar=-1.0, in1=v4(tB),
                                 op0=ALU.mult, op1=ALU.subtract, reads=[btA, btB], writes=[dst_im[1]])
                        else:
                            p.op(e0, "tensor_tensor", out=dst_im[0], in0=v4(tA), in1=v4(tB), op=ALU.add, reads=[btA, btB], writes=[dst_im[1]])
                    _cut(3)
                    PWre, PWim = big["PWre"], big["PWim"]
                    cprod((v4(QAre), B_mats), (v4(QAim), B_mats), pw(PWre, 8), pw(PWim, 8), hb(Cre), hb(Cim), neg_im=True)
                    cprod((v4(P_re), bP), (v4(P_im), bP), pw(PWre, 0), pw(PWim, 0), hb(Bb["Bbre"][:]), hb(Bb["Bbim"][:]))
                    k = 0
                    for src, dst in ((P_re, MBre), (P_im, MBim)):
                        for g4 in range(8):
                            for gg in range(4):
                                p.op("pe", "matmul", pT[k % 2][:, gg, :], lhsT=src[:, g4 * 4 + gg, :], rhs=ident[:], start=True, stop=True,
                                     reads=[bP, B_const], writes=[b_pT[k % 2]], inc=(gg == 3))
                            p.op("act", "copy",
                                 out=dst[:, g4 * 4:(g4 + 1) * 4, :], in_=pT[k % 2][:], reads=[b_pT[k % 2]], writes=[B_mats])
                            k += 1
                    cprod((v4(Q8re), bQ), (v4(nQ8im), bQ), pw(PWre, int(os.environ.get('PWC','16'))), pw(PWim, int(os.environ.get('PWC','16'))), hb(Cre), hb(Cim), neg_im=(os.environ.get('NEG','1')=='1'))
                    _cut(5)
                    p.op("dve", "tensor_scalar", out=rm[:, 0:1], in0=expo[:, 0:1], scalar1=1.0 / 7.0, scalar2=None, op0=ALU.mult, reads=[b_in], writes=[b])
                    p.op("dve", "tensor_scalar", out=rm[:, 1:2], in0=expo[:, 7:8], scalar1=1.0 / 7.0, scalar2=None, op0=ALU.mult, reads=[b_in], writes=[b])
                    for dst, src, col in ((tA, P_re, 0), (tB, P_im, 0), (P_re, P_re, 1), (P_im, P_im, 1)):
                        p.op("act", "activation", out=dst[:], in_=src[:], func=AF.Copy, scale=rm[:, col:col + 1],
                             reads=[bP, b, B_mats], writes=[bP])
                    _cut(6)
                    for g2 in range(16):
                        if g2 == int(os.environ.get('G2STOP', '99')):
                            raise _Stop()
                        for gg in range(2):
                            g = g2 * 2 + gg
                            for d, (sre, sim) in enumerate(((tA, tB), (P_re, P_im))):
                                p.op("pe", "matmul", ptz[d][g2 % 2][:, gg, :], lhsT=sre[:, g, :], rhs=Q8re[:, g, :], start=True, stop=False,
                                     reads=[bP, bQ], writes=[b_ptz[g2 % 2]], inc=False)
                                p.op("pe", "matmul", ptz[d][g2 % 2][:, gg, :], lhsT=sim[:, g, :], rhs=nQ8im[:, g, :], start=False, stop=True,
                                     reads=[bP, bQ], writes=[b_ptz[g2 % 2]], inc=True)
                        for gg in range(2 if os.environ.get('NODVE', '0') == '0' else 0):
                            g = g2 * 2 + gg
                            tz = tzt[g % 2]
                            btz = b_tzt[g % 2]
                            p.op("dve", "tensor_tensor", out=tz[:], in0=ptz[0][g2 % 2][:, gg, :], in1=mask[:, 0, :], op=ALU.mult,
                                 reads=[b_ptz[g2 % 2], b_in], writes=[btz])
                            p.op("dve", "scalar_tensor_tensor", out=tz[:], in0=ident[:], scalar=dcol[:, g:g + 1], in1=tz[:],
                                 op0=ALU.mult, op1=ALU.add, reads=[btz, b_in, B_const], writes=[btz])
                            p.op("dve", "tensor_tensor", out=tzu[g % 2][:], in0=ptz[1][g2 % 2][:, gg, :], in1=mask[:, 1, :], op=ALU.mult,
                                 reads=[b_ptz[g2 % 2], b_in], writes=[b_tzu[g % 2]])
                            p.op("dve", "tensor_tensor", out=Tz[:, g, :], in0=tz[:], in1=tzu[g % 2][:], op=ALU.add,
                                 reads=[btz, b_tzu[g % 2]], writes=[B_mats])
                except _Stop:
                    pass
                for name, t in (("MBre", MBre), ("MBim", MBim), ("QAre", QAre), ("QAim", QAim), ("Tz", Tz)):
                    if name in dbg:
                        p.op("dve", "tensor_copy", out=Q8re[:], in_=t[:], reads=[B_mats, bQ], writes=[bQ])
                        p.dma("sp", dbg[name], Q8re[:].rearrange("p g n -> p (g n)"), reads=[bQ], key=Buf("k"))
                p.barrier()
                p.emit()

        def phase_m1():
            K.rs_mode = "sqrt"
            A = Arena(ARENA0, OFF_MATS)
            wu = A.al("wu", [128, 8, 512], BF16)
            hT = A.al("hT", [128, 8, 8, 128], BF16)
            junk = A.al("junk", [128, D], BF16)
            ssq = A.al("ssq", [128, 8], F32)
            rstd = A.al("rstd", [128, 8], F32)
            xts = [sb("xtm0", [128, 8, D], F32, OFF_YTM), A.al("xtm1", [128, 8, D], F32)]
            XA = Arena(XFREE, SB_TOP)
            xnb = XA.al("xnb", [128, 8, D], BF16)
            Utm = XA.al("Utm", [128, NG, 8, 16], BF16)
            with contextlib.ExitStack() as ps:
                ptr = [ps.enter_context(nc.psum_tensor("ptrm%d" % i, [128, 8, 128], BF16)) for i in range(2)]
                pU = [ps.enter_context(nc.psum_tensor("pU%d" % i, [128, 512], F32)) for i in range(2)]
                pX = [ps.enter_context(nc.psum_tensor("pX%d" % i, [128, 4, 128], BF16)) for i in range(2)]
                b_wu = Buf("wu")
                b_xt = [Buf("xt0"), Buf("xt1")]
                b_st, b_xnb, b_hT, b_U, b_X = (Buf(n) for n in ("st", "xnb", "hT", "Utm", "X"))
                b_ptr = [Buf("ptr0"), Buf("ptr1")]
                b_pU = [Buf("pU0"), Buf("pU1")]
                b_pX = [Buf("pX0"), Buf("pX1")]
                K.b_X = b_X
                for kc in range(8):
                    p.dma("pool", wu[:, kc, :], I.w_in[kc * 128:(kc + 1) * 128, 1536:2048], writes=[b_wu])
                tl = [(c1s, 32, Xc, 0, 1)] + [(x1s[i * 1024:(i + 1) * 1024, :], 128, Xx, i * 128, 0) for i in range(4)]

                def load(i):
                    src, npart = tl[i][0], tl[i][1]
                    p.dma("sp", xts[i % 2][:npart, :, :], src.rearrange("(c i) d -> c i d", i=8), writes=[b_xt[i % 2]])
                def mbufs(i):
                    return (junk, ssq, rstd, xnb, hT, ptr, b_xt[i % 2], b_st, b_xnb, b_hT, b_ptr)
                load(0)
                norm_T(xts[0], tl[0][1], 8, gsm, 3, tl[0][4], mbufs(0))
                for i, (src, npart, Xdst, col0, t) in enumerate(tl):
                    steps = []
                    if i + 1 < len(tl):
                        load(i + 1)
                        steps = norm_A_steps(xts[(i + 1) % 2], tl[i + 1][1], 8, mbufs(i + 1))
                    for ii in range(8):
                        for kc in range(8):
                            p.op("pe", "matmul", pU[ii % 2][:npart, :], lhsT=hT[:, kc, ii, :npart], rhs=wu[:, kc, :],
                                 start=(kc == 0), stop=(kc == 7), reads=[b_hT, b_wu], writes=[b_pU[ii % 2]], inc=(kc == 7))
                        if False:
                            p.op("act", "copy", out=Utm[:npart, :, ii, :], in_=pU[ii % 2][:npart, :].rearrange("p (g h) -> p g h", h=16),
                                 reads=[b_pU[ii % 2]], writes=[b_U], nowaw=True)
                        else:
                            p.op("dve", "tensor_copy", out=Utm[:npart, :, ii, :], in_=pU[ii % 2][:npart, :].rearrange("p (g h) -> p g h", h=16),
                                 reads=[b_pU[ii % 2]], writes=[b_U], nowaw=True)
                        for _ in range(3):
                            if steps:
                                steps.pop(0)()
                    while steps:
                        steps.pop(0)()
                    if i + 1 < len(tl):
                        norm_B(tl[i + 1][1], 8, gsm, 3, tl[i + 1][4], mbufs(i + 1))
                    for g4 in range(8):
                        px = pX[g4 % 2]
                        for gg in range(4):
                            g = g4 * 4 + gg
                            p.op("pe", "transpose", out=px[:, gg, :npart], in_=Utm[:npart, g, :, :].rearrange("p i h -> p (i h)"),
                                 identity=identb[:npart, :npart], reads=[b_U, B_const], writes=[b_pX[g4 % 2]], inc=(gg == 3))
                        if g4 % 4 == 0:
                            p.op("act", "copy", out=Xdst[:, g4 * 4:(g4 + 1) * 4, col0:col0 + npart], in_=px[:, :, :npart],
                                 reads=[b_pX[g4 % 2]], writes=[b_X], nowaw=True)
                        else:
                            p.op("dve", "tensor_copy", out=Xdst[:, g4 * 4:(g4 + 1) * 4, col0:col0 + npart], in_=px[:, :, :npart],
                                 reads=[b_pX[g4 % 2]], writes=[b_X], nowaw=True)
                if "Xx" in dbg:
                    p.barrier()
                    p.emit()
                    xf = sb("xf", [128, 16, 512], F32, OFF_YTM)
                    bxf = Buf("xf")
                    for h in range(2):
                        p.op("dve", "tensor_copy", out=xf[:], in_=Xx[:, h * 16:(h + 1) * 16, :], reads=[b_X, bxf], writes=[bxf])
                        p.dma("sp", dbg["Xx"][:, h * 8192:(h + 1) * 8192], xf[:].rearrange("p g c -> p (g c)"), reads=[bxf], key=bxf)
                    p.op("dve", "tensor_copy", out=xf[:, 0:2, :].rearrange("p a (b c) -> p (a b) c", b=16), in_=Xc[:], reads=[b_X, bxf], writes=[bxf])
                    p.dma("sp", dbg["Xc"], xf[:, 0:2, :].rearrange("p a c -> p (a c)"), reads=[bxf], key=bxf)
                p.barrier()
                p.emit()

        def phase_s5():
            A = Arena(ARENA0, OFF_MATS)
            NS = 544
            sets = []
            for i in range(2):
                d_ = {n: A.al(n + str(i), [128, NS], F32) for n in ("SUre", "SUim", "Gre", "Gim", "Zre", "Zim")}
                d_["HKre"] = A.al("HKre%d" % i, [128, 512], BF16)
                d_["HKim"] = A.al("HKim%d" % i, [128, 512], BF16)
                d_["b"] = {n: Buf(n) for n in ("SU", "Gre", "Gim", "Zre", "Zim", "HK")}
                sets.append(d_)
            phs = []
            for i in range(3):
                phs.append({"ts": A.al("ts%d" % i, [128, NS], F32), "tc": A.al("tc%d" % i, [128, NS], F32), "b": Buf("ph%d" % i)})
            sh = {n: A.al(n, [128, NS], F32) for n in ("tq", "k", "tq2", "k2", "m1", "m2", "m3", "m4")}
            b_sh = {n: Buf(n) for n in sh}
            with contextlib.ExitStack() as ps:
                pS = [[ps.enter_context(nc.psum_tensor("pS%d%d" % (i, j), [128, 512], F32)) for j in range(2)] for i in range(2)]
                pSc = [ps.enter_context(nc.psum_tensor("pSc%d" % i, [128, 64], F32)) for i in range(2)]
                pY = [ps.enter_context(nc.psum_tensor("pY%d" % i, [128, 4, 128], F32)) for i in range(2)]
                b_pS = [Buf("pS0"), Buf("pS1")]
                b_pY = [Buf("pY0"), Buf("pY1")]
                b_X = Buf("Xs5")
                b_Ytm = Buf("Ytm")

                def rev(ap, n):
                    return bass.AP(ap.tensor, ap.offset + (n - 1), [list(ap.ap[0]), [-1, n]])

                def stage_p(g):
                    P_ = phs[g % 3]
                    bp_ = P_["b"]
                    tq, kk, tq2, k2 = sh["tq"], sh["k"], sh["tq2"], sh["k2"]
                    Mb = _mk(magt[:, 0:1], [[0, NS]])
                    p.op("act", "activation", out=tq[:], in_=cidx[:], func=AF.Copy, scale=psi[:, g:g + 1], reads=[B_mats], writes=[b_sh["tq"]])
                    p.op("act", "activation", out=tq2[:], in_=cidx[:], func=AF.Identity, scale=psi[:, g:g + 1], bias=magt[:, 2:3],
                         reads=[B_mats, B_const], writes=[b_sh["tq2"]])
                    p.op("act", "activation", out=k2[:], in_=tq2[:], func=AF.Identity, bias=magt[:, 0:1], reads=[b_sh["tq2"], B_const], writes=[b_sh["k2"]])
                    p.op("act", "activation", out=k2[:], in_=k2[:], func=AF.Identity, bias=magt[:, 1:2], reads=[b_sh["k2"], B_const], writes=[b_sh["k2"]])
                    p.op("pool", "tensor_tensor", out=kk[:], in0=tq[:], in1=Mb, op=ALU.add, reads=[b_sh["tq"], B_const], writes=[b_sh["k"]])
                    p.op("pool", "tensor_tensor", out=kk[:], in0=kk[:], in1=Mb, op=ALU.subtract, reads=[b_sh["k"], B_const], writes=[b_sh["k"]])
                    p.op("pool", "tensor_tensor", out=P_["ts"][:], in0=tq[:], in1=kk[:], op=ALU.subtract, reads=[b_sh["tq"], b_sh["k"]], writes=[bp_])
                    p.op("pool", "tensor_tensor", out=P_["tc"][:], in0=tq2[:], in1=k2[:], op=ALU.subtract, reads=[b_sh["tq2"], b_sh["k2"]], writes=[bp_])
                    p.op("act", "activation", out=P_["ts"][:], in_=P_["ts"][:], func=AF.Sin, scale=TWO_PI, reads=[bp_], writes=[bp_])
                    p.op("act", "activation", out=P_["tc"][:], in_=P_["tc"][:], func=AF.Sin, scale=TWO_PI, reads=[bp_], writes=[bp_])

                def stage_a(g):
                    S = sets[g % 2]
                    bs = S["b"]
                    ps_re, ps_im = pS[g % 2]
                    psc = pSc[g % 2]
                    bp = b_pS[g % 2]
                    p.op("pe", "matmul", ps_re[:, :], lhsT=MBre[:, g, :], rhs=Xx[:, g, :], start=True, stop=True, reads=[B_mats, b_X], writes=[bp], inc=False)
                    p.op("pe", "matmul", ps_im[:, :], lhsT=MBim[:, g, :], rhs=Xx[:, g, :], start=True, stop=True, reads=[B_mats, b_X], writes=[bp], inc=False)
                    p.op("pe", "matmul", psc[:, 0:32], lhsT=MBre[:, g, :], rhs=Xc[:, g, :], start=True, stop=True, reads=[B_mats, b_X], writes=[bp], inc=False)
                    p.op("pe", "matmul", psc[:, 32:64], lhsT=MBim[:, g, :], rhs=Xc[:, g, :], start=True, stop=True, reads=[B_mats, b_X], writes=[bp], inc=True)
                    for nm, pss, c0 in (("SUre", ps_re, 0), ("SUim", ps_im, 32)):
                        SU = S[nm]
                        p.op("act", "copy", out=SU[0:64, 0:32], in_=psc[0:64, c0:c0 + 32], reads=[bp], writes=[bs["SU"]])
                        p.op("act", "copy", out=SU[0:64, 32:544], in_=pss[0:64, :], reads=[bp], writes=[bs["SU"]])
                        p.op("act", "copy", out=SU[64:128, 0:32], in_=rev(psc[64:128, c0:c0 + 32], 32), reads=[bp], writes=[bs["SU"]])
                        p.op("act", "copy", out=SU[64:128, 32:544], in_=rev(pss[64:128, :], 512), reads=[bp], writes=[bs["SU"]])

                def stage_b(g):
                    S = sets[g % 2]
                    bs = S["b"]
                    P_ = phs[g % 3]
                    sn, cs, bph = P_["ts"], P_["tc"], P_["b"]
                    m = [sh["m1"], sh["m2"], sh["m3"], sh["m4"]]
                    bm = [b_sh["m1"], b_sh["m2"], b_sh["m3"], b_sh["m4"]]

                    def mul(i, a, bsrc, bb, eng):
                        p.op(eng, "tensor_tensor", out=m[i][:], in0=a[:], in1=bb[:], op=ALU.mult, reads=[bsrc, bph], writes=[bm[i]])
                    mul(0, S["SUre"], bs["SU"], cs, "dve")
                    mul(1, S["SUim"], bs["SU"], sn, "dve")
                    mul(2, S["SUim"], bs["SU"], cs, "dve")
                    mul(3, S["SUre"], bs["SU"], sn, "dve")
                    p.op("dve", "tensor_tensor", out=S["Gre"][:], in0=m[0][:], in1=m[1][:], op=ALU.add, reads=[bm[0], bm[1]], writes=[bs["Gre"]])
                    p.op("dve", "tensor_tensor", out=S["Gim"][:], in0=m[2][:], in1=m[3][:], op=ALU.subtract, reads=[bm[2], bm[3]], writes=[bs["Gim"]])
                    rb = _mk(rcol[:, g:g + 1], [[0, NS]])
                    p.op("dve", "tensor_tensor_scan", out=S["Zre"][:], data0=rb, data1=S["Gre"][:], initial=0.0, op0=ALU.mult, op1=ALU.add,
                         reads=[bs["Gre"], B_mats], writes=[bs["Zre"]])
                    p.op("dve", "tensor_tensor_scan", out=S["Zim"][:], data0=rb, data1=S["Gim"][:], initial=0.0, op0=ALU.mult, op1=ALU.add,
                         reads=[bs["Gim"], B_mats], writes=[bs["Zim"]])
                    mul(0, S["Zre"], bs["Zre"], cs, "dve")
                    mul(1, S["Zim"], bs["Zim"], sn, "dve")
                    mul(2, S["Zim"], bs["Zim"], cs, "dve")
                    mul(3, S["Zre"], bs["Zre"], sn, "dve")
                    for nm, hn, ia, ib, op in (("HKre", "Gre", 0, 1, ALU.subtract), ("HKim", "Gim", 2, 3, ALU.add)):
                        HK = S[nm]
                        Hh = S[hn]
                        p.op("dve", "tensor_tensor", out=Hh[:], in0=m[ia][:], in1=m[ib][:], op=op, reads=[bm[ia], bm[ib]], writes=[bs[hn]])
                        p.op("act", "copy", out=HK[0:64, :], in_=Hh[0:64, 31:543], reads=[bs[hn]], writes=[bs["HK"]])
                        p.op("act", "copy", out=rev(HK[64:128, :], 512), in_=Hh[64:128, 31:543], reads=[bs[hn]], writes=[bs["HK"]])

                def stage_c(g):
                    S = sets[g % 2]
                    bs = S["b"]
                    py = pY[g % 2]
                    for cb in range(4):
                        cs_ = slice(cb * 128, (cb + 1) * 128)
                        p.op("pe", "matmul", py[:, cb, :], lhsT=Xx[:, g, cs_], rhs=Tz[:, g, :], start=True, stop=False,
                             reads=[b_X, B_mats], writes=[b_pY[g % 2]], inc=False)
                        p.op("pe", "matmul", py[:, cb, :], lhsT=S["HKre"][:, cs_], rhs=QAre[:, g, :], start=False, stop=False,
                             reads=[bs["HK"], B_mats], writes=[b_pY[g % 2]], inc=False)
                        p.op("pe", "matmul", py[:, cb, :], lhsT=S["HKim"][:, cs_], rhs=QAim[:, g, :], start=False, stop=True,
                             reads=[bs["HK"], B_mats], writes=[b_pY[g % 2]], inc=(cb == 3))
                    p.op("act", "copy", out=Ytm[:, :, :, g * 16:(g + 1) * 16], in_=py[:, :, :].rearrange("p c (j h) -> p c j h", h=16),
                         reads=[b_pY[g % 2]], writes=[b_Ytm])
                K.wc_pre = None
                if "m2a" in phases:
                    K.wc_pre = Buf("wc_pre")
                    wc_p = sb("wc_pre", [128, 8, 1536], BF16, XFREE)
                    for kc in range(8):
                        p.dma("pool", wc_p[:, kc, :], I.w_in[kc * 128:(kc + 1) * 128, 0:1536], writes=[K.wc_pre])
                stage_p(0)
                stage_p(1)
                stage_a(0)
                for g in range(NG):
                    if g + 2 < NG:
                        stage_p(g + 2)
                    if g + 1 < NG:
                        stage_a(g + 1)
                    stage_b(g)
                    stage_c(g)
                if "Ytm" in dbg:
                    XA = Arena(ARENA0, OFF_MATS)
                    p.barrier()
                    p.emit()
                    tmpf = XA.al("ytmf", [128, 8, 512], F32)
                    bt = Buf("ytmf")
                    for cb in range(4):
                        p.op("dve", "tensor_copy", out=tmpf[:], in_=Ytm[:, cb], reads=[b_Ytm, bt], writes=[bt])
                        p.dma("sp", dbg["Ytm"][cb * 128:(cb + 1) * 128, :], tmpf[:].rearrange("p j c -> p (j c)"), reads=[bt], key=bt)
                p.barrier(keep=([K.wc_pre] if K.wc_pre is not None else []))
                p.emit()

        def phase_m2a():
            K.rs_mode = "sqrt"
            A = Arena(ARENA0, OFF_BGS)
            wc_pre = getattr(K, "wc_pre", None)
            wc = sb("wc", [128, 8, 1536], BF16, XFREE) if wc_pre is not None else A.al("wc", [128, 8, 1536], BF16)
            xts = [A.al("xt", [128, 2, D], F32) for _ in range(3)]
            xnb = A.al("xnb", [128, 2, D], BF16)
            hTs = [A.al("hT", [128, 8, 2, 128], BF16) for _ in range(2)]
            cgs = [A.al("cgs", [128, 256], F32) for _ in range(2)]
            junk = A.al("junk", [128, D], BF16)
            ssq = A.al("ssq", [128, 8], F32)
            rstd = A.al("rstd", [128, 8], F32)
            with contextlib.ExitStack() as ps:
                ptr = [ps.enter_context(nc.psum_tensor("ptrc%d" % i, [128, 8, 128], BF16)) for i in range(2)]
                pq = [[ps.enter_context(nc.psum_tensor("pq%d%d" % (i, j), [128, 512], F32)) for j in range(3)] for i in range(2)]
                b_wc = [Buf("wc%d" % k) for k in range(8)]
                b_xt = [Buf("xt0"), Buf("xt1"), Buf("xt2")]
                b_st, b_xnb, b_z, b_bg = (Buf(n) for n in ("st", "xnb", "z", "bg"))
                b_hTs = [Buf("hT0"), Buf("hT1")]
                b_ptr = [Buf("ptr0"), Buf("ptr1")]
                b_pq = [[Buf("pq") for j in range(3)] for i in range(2)]
                b_cgs = [Buf("cgs0"), Buf("cgs1")]
                if wc_pre is not None:
                    b_wc = [wc_pre] * 8
                else:
                    for kc in range(8):
                        p.dma("pool", wc[:, kc, :], I.w_in[kc * 128:(kc + 1) * 128, 0:1536], writes=[b_wc[kc]])
                NT = 16

                def load(i):
                    p.dma("sp", xts[i % 3][:, :, :], x1s[i * 256:(i + 1) * 256, :].rearrange("(s p) d -> p s d", p=128), writes=[b_xt[i % 3]])

                def nbufs(i):
                    return (junk, ssq, rstd, xnb, hTs[i % 2], ptr, b_xt[i % 3], b_st, b_xnb, b_hTs[i % 2], b_ptr)
                load(0)
                norm_T(xts[0], 128, 2, gsm, 3, 0, nbufs(0))
                k = 0
                for i in range(NT):
                    stepsA, stepsB = [], []
                    if i + 1 < NT:
                        load(i + 1)
                        stepsA = norm_A_steps(xts[(i + 1) % 3], 128, 2, nbufs(i + 1))
                        stepsB = norm_B_steps(128, 2, gsm, 3, 0, nbufs(i + 1))
                    tok = slice(i * 256, (i + 1) * 256)
                    hTf = hTs[i % 2][:].rearrange("p k s c -> p k (s c)")
                    b_hT = b_hTs[i % 2]
                    for q in range(4):
                        pp = pq[k % 2]
                        bp = b_pq[k % 2]
                        for j, c0 in enumerate((512 + q * 128, 1024 + q * 128, q * 128)):
                            for kc in range(8):
                                p.op("pe", "matmul", pp[j][:, :256], lhsT=wc[:, kc, c0:c0 + 128], rhs=hTf[:, kc, :],
                                     start=(kc == 0), stop=(kc == 7), reads=[b_wc[kc], b_hT], writes=[bp[j]], inc=(kc == 7))
                            if q >= 1 and stepsB and not stepsA:
                                stepsB.pop(0)()
                        p.op("dve", "tensor_copy", out=cgs[k % 2][:, :], in_=pp[0][:, :256], reads=[bp[0]], writes=[b_cgs[k % 2]])
                        p.op("dve", "tensor_copy", out=bgs[:, q, tok], in_=pp[2][:, :256], reads=[bp[2]], writes=[b_bg])
                        p.op("dve", "tensor_tensor", out=ymix[:, q, tok], in0=cgs[k % 2][:, :], in1=pp[1][:, :256], op=ALU.mult,
                             reads=[b_cgs[k % 2], bp[1]], writes=[b_z])
                        k += 1
                        if q == 0:
                            while stepsA:
                                stepsA.pop(0)()
                    while stepsA:
                        stepsA.pop(0)()
                    while stepsB:
                        stepsB.pop(0)()
                p.barrier()
                p.emit()

        def phase_m2b():
            A = Arena(ARENA0, OFF_BGS)
            acc = [A.al("acc", [128, L], F32) for _ in range(2)]
            tsh = [A.al("tsh", [128, L], F32) for _ in range(2)]
            b_acc = [Buf("acc0"), Buf("acc1")]
            b_tsh = [Buf("tsh0"), Buf("tsh1")]
            b_z = [Buf("z%d" % q) for q in range(4)]
            b_bg = Buf("bg")
            tv = [t_[:].rearrange("p (r c) -> p r c", c=64) for t_ in tsh]
            p.op("pool", "memset", tv[0][:, :, 0:1], 0.0, writes=[b_tsh[0]])
            p.op("pool", "memset", tv[1][:, :, 63:64], 0.0, writes=[b_tsh[1]])
            for q in range(4):
                a = acc[q % 2]
                ba = b_acc[q % 2]
                z = ymix[:, q, :]
                w0, w1, w2 = (cwt[:, q, j:j + 1] for j in range(3))
                p.op("act", "activation", out=a[:], in_=z, func=AF.Copy, scale=w1, reads=[b_z[q], B_const], writes=[ba])
                if q < 2:
                    zv = z.rearrange("p (r c) -> p r c", c=64)
                    p.op("act", "activation", out=tv[0][:, :, 1:64], in_=zv[:, :, 0:63], func=AF.Copy, scale=w0,
                         reads=[b_z[q], B_const], writes=[b_tsh[0]], nowaw=True)
                    p.op("act", "activation", out=tv[1][:, :, 0:63], in_=zv[:, :, 1:64], func=AF.Copy, scale=w2,
                         reads=[b_z[q], B_const], writes=[b_tsh[1]], nowaw=True)
                    p.op("dve", "tensor_tensor", out=a[:], in0=a[:], in1=tsh[0][:], op=ALU.add, reads=[ba, b_tsh[0]], writes=[ba])
                    p.op("dve", "tensor_tensor", out=a[:], in0=a[:], in1=tsh[1][:], op=ALU.add, reads=[ba, b_tsh[1]], writes=[ba])
                else:
                    lo_out, lo_in = a[:, 64:L], z[:, 0:L - 64]
                    hi_out, hi_in = a[:, 0:L - 64], z[:, 64:L]
                    p.op("dve", "scalar_tensor_tensor", out=lo_out, in0=lo_in, scalar=w0, in1=lo_out, op0=ALU.mult, op1=ALU.add,
                         reads=[b_z[q], ba, B_const], writes=[ba])
                    p.op("dve", "scalar_tensor_tensor", out=hi_out, in0=hi_in, scalar=w2, in1=hi_out, op0=ALU.mult, op1=ALU.add,
                         reads=[b_z[q], ba, B_const], writes=[ba])
                p.op("dve" if q < 2 else "pool", "tensor_tensor", out=z, in0=bgs[:, q, :], in1=a[:], op=ALU.mult,
                     reads=[b_bg, ba], writes=[b_z[q]])
            p.barrier()
            p.emit()

        def phase_m2c():
            A = Arena(ARENA0, OFF_YTM)
            ys = A.al("ys", [128, 4, L], BF16)
            pre_wo = ("m2d" in phases) and ("ffn2" in phases)
            if pre_wo:
                wo_off = ARENA0 + 78848
                wo_p = sb("wo_pre", [128, 8, D], BF16, wo_off)
            wgl = A.al("wgl", [128, 4, 512], BF16)
            sgl = [A.al("sgl", [128, 512], F32) for _ in range(2)]
            assert (not pre_wo) or A.ptr <= ARENA0 + 78848
            with contextlib.ExitStack() as ps:
                pt2 = [ps.enter_context(nc.psum_tensor("pt2%d" % i, [128, 8, 128], BF16)) for i in range(2)]
                pgl = [ps.enter_context(nc.psum_tensor("pgl%d" % i, [128, 512], F32)) for i in range(2)]
                b_pt2 = [Buf("pt20"), Buf("pt21")]
                b_pgl = [Buf("pgl0"), Buf("pgl1")]
                b_sgl = [Buf("sgl0"), Buf("sgl1")]
                b_ys, b_w, b_Y, b_o = Buf("ys"), Buf("wgl"), Buf("Ytm"), Buf("yssm")
                for m in range(4):
                    p.dma("pool", wgl[:, m, :], I.w_glu[m * 128:(m + 1) * 128, :], writes=[b_w])
                K.wo_pre = None
                if pre_wo:
                    K.wo_pre = Buf("wo_pre")
                    for kc in range(8):
                        p.dma("pool", wo_p[:, kc, :], I.w_out[kc * 128:(kc + 1) * 128, :], writes=[K.wo_pre])
                k = 0
                for cb in range(4):
                    for m in range(4):
                        pt = pt2[k % 2]
                        for j in range(8):
                            p.op("pe", "transpose", out=pt[:, j, :], in_=Ytm[:, cb, j, m * 128:(m + 1) * 128], identity=identb[:],
                                 reads=[b_Y, B_const], writes=[b_pt2[k % 2]], inc=(j == 7))
                        src = _mk(pt[:, 0, 0:1], [[1, 128], [128, 8]])
                        p.op("act", "activation", out=ys[:, m, cb * 1024:(cb + 1) * 1024].rearrange("p (c j) -> p c j", j=8), in_=src,
                             func=AF.Gelu_apprx_tanh, reads=[b_pt2[k % 2]], writes=[b_ys])
                        k += 1
                k = 0
                for tb in range(8):
                    tok = slice(tb * 512, (tb + 1) * 512)
                    for co in range(4):
                        pg_ = pgl[k % 2]
                        for m in range(4):
                            p.op("pe", "matmul", pg_[:, :], lhsT=wgl[:, m, co * 128:(co + 1) * 128], rhs=ys[:, m, tok],
                                 start=(m == 0), stop=(m == 3), reads=[b_w, b_ys], writes=[b_pgl[k % 2]], inc=(m == 3))
                        p.op("act", "activation", out=sgl[k % 2][:, :], in_=pg_[:, :], func=AF.Sigmoid, bias=bglut[:, co:co + 1],
                             reads=[b_pgl[k % 2], B_const], writes=[b_sgl[k % 2]])
                        p.op("dve", "tensor_tensor", out=ymix[:, 4 + co, tok], in0=ys[:, co, tok], in1=sgl[k % 2][:, :], op=ALU.mult,
                             reads=[b_ys, b_sgl[k % 2]], writes=[b_o])
                        k += 1
                if "ymix" in dbg:
                    p.barrier()
                    p.emit()
                    yf = A.al("yf", [128, 2, L], F32)
                    byf = Buf("yf")
                    for h in range(4):
                        p.op("dve", "tensor_copy", out=yf[:], in_=ymix[:, h * 2:(h + 1) * 2, :], reads=[byf], writes=[byf])
                        p.dma("sp", dbg["ymix"][:, h * 2 * L:(h + 1) * 2 * L], yf[:].rearrange("p a t -> p (a t)"), reads=[byf], key=byf)
                p.barrier(keep=([K.wo_pre] if K.wo_pre is not None else []))
                p.emit()

        def phase_m2d():
            K.rs_mode = "sqrt"
            pre = "ffn2" in phases
            A = Arena(ARENA0 + (78848 if pre else 0), OFF_YMIX)
            wo = A.al("wo", [128, 8, D], BF16)
            xts = [A.al("xt", [128, 2, D], F32) for _ in range(3)]
            tmp = [A.al("tmp", [128, 512], F32) for _ in range(2)]
            dg = [A.al("dg", [128, 128], F32) for _ in range(2)]
            junk = A.al("junk", [128, D], BF16)
            sty = A.al("sty", [128, 16], F32)
            Gt = A.al("G", [128, D], F32)
            with contextlib.ExitStack() as ps:
                pys = [ps.enter_context(nc.psum_tensor("pyo%d" % i, [128, D], F32)) for i in range(3)]
                b_wo = Buf("wo")
                b_xt = [Buf("xt0"), Buf("xt1"), Buf("xt2")]
                b_py = [Buf("py0"), Buf("py1"), Buf("py2")]
                b_tmp = [Buf("tmp0"), Buf("tmp1")]
                bdg = [Buf("dg0"), Buf("dg1")]
                b_sty, b_junk, b_Gt, b_ym = Buf("sty"), Buf("junk"), Buf("G"), Buf("ymix")
                b_sty2 = [Buf("sty0"), Buf("sty1")]
                if getattr(K, "wo_pre", None) is not None:
                    b_wo = K.wo_pre
                else:
                    for kc in range(8):
                        p.dma("pool", wo[:, kc, :], I.w_out[kc * 128:(kc + 1) * 128, :], writes=[b_wo])
                K.ffn2_pre = None
                pre_dmas = []
                if pre:
                    K.ffn2_pre, pre_dmas = ffn_weight_dmas(I.f2_wgu, I.f2_wd, range(7), False, deferred=True)
                make_row(Gt, lambda kc: gcm[:, kc, 0:1], b_Gt, pys[0], b_py[0], dg, bdg)
                NT = 16

                def load(i):
                    p.dma("sp", xts[i % 3][:, :, :], x1s[i * 256:(i + 1) * 256, :].rearrange("(s p) d -> p s d", p=128), writes=[b_xt[i % 3]])
                load(0)
                k = 0
                for i in range(NT):
                    if i + 1 < NT:
                        load(i + 1)
                    xt = xts[i % 3]
                    bx = b_xt[i % 3]
                    for s_ in range(2):
                        t0 = i * 256 + s_ * 128
                        py = pys[k % 3]
                        for h in range(2):
                            for m in range(8):
                                p.op("pe", "matmul", py[:, h * 512:(h + 1) * 512], lhsT=ymix[:, m, t0:t0 + 128], rhs=wo[:, m, h * 512:(h + 1) * 512],
                                     start=(m == 0), stop=(m == 7), reads=[b_ym, b_wo], writes=[b_py[k % 3]], inc=(m == 7))
                        post_norm_add(py, b_py[k % 3], xt[:, s_, :], bx, Gt, b_Gt, tmp, b_tmp, sty, b_sty2[s_], junk, b_junk, s_,
                                      add_engs=("dve", "dve"))
                        k += 1
                    if pre_dmas and i % 2 == 1:
                        pre_dmas.pop(0)()
                    p.dma("pool", x2s[i * 256:(i + 1) * 256, :].rearrange("(s p) d -> p s d", p=128), xt[:, :, :], reads=[bx], key=bx)
                while pre_dmas:
                    pre_dmas.pop(0)()
                p.barrier(keep=([K.ffn2_pre] if K.ffn2_pre is not None else []))
                p.emit()

        K.phase_extra = {"s5setup": phase_s5setup, "m1": phase_m1, "s5": phase_s5, "m2a": phase_m2a, "m2b": phase_m2b,
                         "m2c": phase_m2c, "m2d": phase_m2d}

        run = [ph for ph in all_ph if ph in phases]
        phase_init()
        xtiles = lambda src, dst: [(src[i * 256:(i + 1) * 256, :], dst[i * 256:(i + 1) * 256, :], 2, 0) for i in range(16)]
        for ph in run:
            if ph == "mods":
                phase_mods()
                p.barrier(keep=([K.ffn1_wd] if getattr(K, "ffn1_wd", None) is not None else []))
                p.emit()
            elif ph == "ffn1":
                phase_ffn("a", I.f1_wgu, I.f1_wd, gs1, 0, gc1, [(I.ctx, c1s, 2, 1)] + xtiles(I.x, x1s),
                          pre_kcs=(range(8) if "mods" in phases else ()), pre_wd=("mods" in phases),
                          pre_buf_wd=(getattr(K, "ffn1_wd", None) if "mods" in phases else None))
            elif ph == "ffn2":
                phase_ffn("b", I.f2_wgu, I.f2_wd, gs2, 6, gc2, xtiles(x2s, out),
                          pre_kcs=(range(7) if "m2d" in phases else ()), pre_wd=False,
                          pre_buf=(getattr(K, "ffn2_pre", None) if "m2d" in phases else None))
            else:
                K.phase_extra[ph]()
        with contextlib.ExitStack() as ps:
            fin = Buf("fin")
            if dbg:
                for name, src in (("x1s", x1s), ("c1s", c1s), ("x2s", x2s)):
                    if name in dbg:
                        p.dma("sp", dbg[name], src, writes=[fin], key=fin)
            p.barrier()
            p.emit()
    return nc


K_phase_doc = None


def prep_inputs(inputs, b):
    f = lambda k: np.ascontiguousarray(np.asarray(inputs[k], dtype=np.float32))
    col = lambda v: np.ascontiguousarray(v.reshape(-1, 128).T)
    m = {}
    m["x"] = f("x")[b]
    m["ctx"] = f("ctx")[b]
    cc = np.stack([col(f("c")[b]), col(f("c_ctx"))], axis=-1)
    m["cc"] = np.ascontiguousarray(cc.reshape(128, 16))
    m["w_ada"] = f("w_ada")[0]
    m["bada"] = col(f("b_ada")[0])
    g6 = np.stack([col(f(k)[0]) for k in ("ffn1_g_pre", "ffn1_g_post", "mix_g_pre", "mix_g_post", "ffn2_g_pre", "ffn2_g_post")], axis=-1)
    m["gains"] = np.ascontiguousarray(g6.reshape(128, 48))
    m["ffn1_w_gu"] = f("ffn1_w_gu")[0]
    m["ffn1_w_down"] = f("ffn1_w_down")[0]
    m["ffn2_w_gu"] = f("ffn2_w_gu")[0]
    m["ffn2_w_down"] = f("ffn2_w_down")[0]
    m["w_in"] = f("w_in")[0]
    cw = f("conv_w")[0]
    m["cw"] = np.ascontiguousarray(np.stack([col(cw[k]) for k in range(3)], axis=-1).reshape(128, 12))
    dp = lambda a: np.ascontiguousarray(np.transpose(a, (0, 2, 1)).reshape(128, NG))
    logdt = np.broadcast_to(f("ssm_log_dt")[0][:, :, None], (2, NG, 64))
    m["s5a"] = np.ascontiguousarray(np.stack([dp(f("ssm_lam_re")[0]), dp(f("ssm_lam_im")[0]), dp(logdt)], axis=1).reshape(128, 96))
    bre = np.transpose(f("ssm_b_re")[0], (0, 2, 1, 3)).reshape(128, NG, 16)
    bim = np.transpose(f("ssm_b_im")[0], (0, 2, 1, 3)).reshape(128, NG, 16)
    cre = np.transpose(f("ssm_c_re")[0], (0, 3, 1, 2)).reshape(128, NG, 16)
    cim = np.transpose(f("ssm_c_im")[0], (0, 3, 1, 2)).reshape(128, NG, 16)
    m["s5b"] = np.ascontiguousarray(np.stack([bre, bim, cre, cim], axis=1).reshape(128, 4 * NG * 16))
    dsk = f("ssm_d")[0].reshape(NG, 16)
    m["dcol"] = np.ascontiguousarray(np.tile(dsk.T, (8, 1)))
    m["w_glu"] = f("w_glu")[0]
    m["bglu"] = col(f("b_glu")[0])
    m["w_out"] = f("w_out")[0]
    ii = np.arange(128) // 16
    ident = np.eye(128, dtype=np.float32)
    mask0 = (ii[:, None] <= ii[None, :]).astype(np.float32)
    mask1 = (ii[:, None] >= ii[None, :]).astype(np.float32)
    ex = np.zeros((128, 26), np.float32)
    i8 = np.arange(8)
    ex[:64, 0:8] = 7 - i8
    ex[64:, 0:8] = i8
    ex[:64, 8:16] = i8 + 1
    ex[64:, 8:16] = 8 - i8
    ex[:, 16:24] = ex[:, 8:16] - 8
    ex[:, 24] = 8
    ex[:, 25] = 1
    ramp = np.broadcast_to(np.arange(544, dtype=np.float32), (128, 544))
    m["cst"] = np.ascontiguousarray(np.concatenate([ident, mask0, mask1, ex, ramp], axis=1))
    return m


_NC_CACHE = {}


def kernel(**inputs):
    if "nc" not in _NC_CACHE:
        _NC_CACHE["nc"] = build()
    nc = _NC_CACHE["nc"]
    in_maps = [prep_inputs(inputs, b) for b in range(8)]
    res = run_bass_kernel_spmd(nc, in_maps, core_ids=list(range(8)))
    return np.stack([np.asarray(r["out"], dtype=np.float32) for r in res.results], axis=0)
```
